# Optimizing a Trainium2 kernel written in Bass

```python
import math
import jax
import jax.numpy as jnp
from jax import lax
import numpy as np

D_MODEL = 2048
BATCH = 32
SEQ = 256
DEPTH = 2
DEC_BATCH = 2
DEC_SEQ = 1024
PAST_LEN = 512

GRID_W = 64
DK_GDN = 128
DV_GDN = 128
H_GDN = D_MODEL // (2 * DV_GDN)
GDN_W = H_GDN * DV_GDN
HEAD_DIM = 128
N_Q = D_MODEL // (2 * HEAD_DIM)
N_KV = N_Q // 4
ATTN_W = N_Q * HEAD_DIM
MIX_W = GDN_W + ATTN_W
CONV_K = 5
CHUNK = 64
Q_BLOCK = 128
D_FF = ((8 * D_MODEL + 767) // 768) * 256
ROPE_BASE = 10000.0
RMS_EPS = 1e-6
LN_EPS = 1e-5
DN_ALPHA = (2 * DEPTH) ** 0.25
DN_BETA = (8 * DEPTH) ** -0.25
QKV_W = 2 * H_GDN * DK_GDN + GDN_W
IN_W = QKV_W + GDN_W + 4 * H_GDN + ATTN_W + 2 * N_KV * HEAD_DIM

kernel_name = 'hybrid_gdn_gqa_diffusion_step'


def rms_norm(x, w):
    xf = x.astype(jnp.float32)
    y = xf * lax.rsqrt(jnp.mean(xf * xf, axis=-1, keepdims=True) + RMS_EPS)
    return (y * w.astype(jnp.float32)).astype(x.dtype)


def layer_norm(x, g, b):
    xf = x.astype(jnp.float32)
    mu = jnp.mean(xf, axis=-1, keepdims=True)
    var = jnp.mean(jnp.square(xf - mu), axis=-1, keepdims=True)
    y = (xf - mu) * lax.rsqrt(var + LN_EPS)
    return (y * g.astype(jnp.float32) + b.astype(jnp.float32)).astype(x.dtype)


def l2_normalize(x):
    return x * lax.rsqrt(jnp.sum(x * x, axis=-1, keepdims=True) + RMS_EPS)


def axial_rope_tables(n_tok):
    rows = n_tok // GRID_W
    row = jnp.repeat(jnp.arange(rows), GRID_W).astype(jnp.float32)
    col = jnp.tile(jnp.arange(GRID_W), rows).astype(jnp.float32)
    half = HEAD_DIM // 2
    inv_freq = ROPE_BASE ** (-jnp.arange(0, half, 2, dtype=jnp.float32) / half)
    ang_r = row[:, None] * inv_freq
    ang_c = col[:, None] * inv_freq
    return (jnp.cos(ang_r), jnp.sin(ang_r), jnp.cos(ang_c), jnp.sin(ang_c))


def _rotate(x, cos, sin):
    n = x.shape[-1] // 2
    x1 = x[..., :n].astype(jnp.float32)
    x2 = x[..., n:].astype(jnp.float32)
    c = cos[:, None, :]
    s = sin[:, None, :]
    return jnp.concatenate([x1 * c - x2 * s, x2 * c + x1 * s], axis=-1)


def apply_axial_rope(x, tabs):
    cos_r, sin_r, cos_c, sin_c = tabs
    half = HEAD_DIM // 2
    y = jnp.concatenate([_rotate(x[..., :half], cos_r, sin_r),
                         _rotate(x[..., half:], cos_c, sin_c)], axis=-1)
    return y.astype(x.dtype)


def short_conv(x, w):
    pad = CONV_K // 2
    t = x.shape[1]
    xp = jnp.pad(x, ((0, 0), (pad, pad), (0, 0)))
    return sum(xp[:, j:j + t] * w[j] for j in range(CONV_K))


def gated_delta_chunked(q, k, v, g, beta, s0):
    b_, t_, h_, _ = q.shape
    dv = v.shape[-1]
    n = t_ // CHUNK

    def chunks(a):
        a = a.reshape((b_, n, CHUNK, h_) + a.shape[3:])
        return jnp.moveaxis(a, (1, 3), (0, 2))

    qc, kc, vc, gc, bc = chunks(q), chunks(k), chunks(v), chunks(g), chunks(beta)
    gcum = jnp.cumsum(gc, axis=-1)
    idx = jnp.arange(CHUNK)
    tril = idx[:, None] >= idx[None, :]
    strict = idx[:, None] > idx[None, :]
    decay = jnp.exp(jnp.where(tril, gcum[..., :, None] - gcum[..., None, :], -jnp.inf))
    kb = kc * bc[..., None]
    lmat = jnp.where(strict, jnp.einsum('nbhid,nbhjd->nbhij', kb, kc) * decay, 0.0)
    amat = lmat + jnp.eye(CHUNK, dtype=jnp.float32)
    rhs = jnp.concatenate([vc * bc[..., None], kb * jnp.exp(gcum)[..., None]], axis=-1)
    sol = lax.linalg.triangular_solve(amat, rhs, left_side=True, lower=True, unit_diagonal=True)
    u, w = sol[..., :dv], sol[..., dv:]
    attn = jnp.einsum('nbhid,nbhjd->nbhij', qc, kc) * decay

    def step(s, inp):
        q_i, k_i, u_i, w_i, a_i, g_i = inp
        v_new = u_i - jnp.einsum('bhcd,bhde->bhce', w_i, s)
        o = (jnp.einsum('bhcd,bhde->bhce', q_i * jnp.exp(g_i)[..., None], s)
             + jnp.einsum('bhij,bhje->bhie', a_i, v_new))
        g_last = g_i[..., -1]
        s = (s * jnp.exp(g_last)[..., None, None]
             + jnp.einsum('bhcd,bhce->bhde', k_i * jnp.exp(g_last[..., None] - g_i)[..., None], v_new))
        return s, o

    s_fin, o = lax.scan(step, s0, (qc, kc, u, w, attn, gcum))
    o = jnp.moveaxis(o, (0, 2), (1, 3)).reshape(b_, t_, h_, dv)
    return o, s_fin


def gated_deltanet(qkv, z, b, a, conv_w, a_log, dt_bias, norm_w, s0):
    bsz, t_, _ = qkv.shape
    f32 = jnp.float32
    qkv = jax.nn.silu(short_conv(qkv, conv_w)).astype(f32)
    q, k, v = jnp.split(qkv, [H_GDN * DK_GDN, 2 * H_GDN * DK_GDN], axis=-1)
    q = l2_normalize(q.reshape(bsz, t_, H_GDN, DK_GDN)) * DK_GDN ** -0.5
    k = l2_normalize(k.reshape(bsz, t_, H_GDN, DK_GDN))
    v = v.reshape(bsz, t_, H_GDN, DV_GDN)
    beta = jax.nn.sigmoid(b.astype(f32)).reshape(bsz, t_, 2, H_GDN)
    g = -jnp.exp(a_log.astype(f32)) * jax.nn.softplus(
        a.astype(f32).reshape(bsz, t_, 2, H_GDN) + dt_bias.astype(f32))
    s0 = s0.astype(f32)
    o_f, s_f = gated_delta_chunked(q, k, v, g[:, :, 0], beta[:, :, 0], s0[:, 0])
    rev = lambda arr: jnp.flip(arr, axis=1)
    o_b, s_b = gated_delta_chunked(rev(q), rev(k), rev(v), rev(g[:, :, 1]), rev(beta[:, :, 1]), s0[:, 1])
    o = o_f + rev(o_b)
    o = rms_norm(o, norm_w) * jax.nn.silu(z.astype(f32).reshape(bsz, t_, H_GDN, DV_GDN))
    return o.reshape(bsz, t_, GDN_W).astype(z.dtype), jnp.stack([s_f, s_b], axis=1)


def block_attention(q, k, v):
    bsz, t_, _, _ = q.shape
    grp = N_Q // N_KV
    nb = t_ // Q_BLOCK
    qb = jnp.moveaxis(q.reshape(bsz, nb, Q_BLOCK, N_KV, grp, HEAD_DIM), 1, 0)
    scale = HEAD_DIM ** -0.5

    def one_block(q_blk):
        s = jnp.einsum('bqkgd,bskd->bkgqs', q_blk, k).astype(jnp.float32) * scale
        p = jax.nn.softmax(s, axis=-1)
        return jnp.einsum('bkgqs,bskd->bqkgd', p.astype(v.dtype), v)

    o = lax.map(one_block, qb)
    return jnp.moveaxis(o, 0, 1).reshape(bsz, t_, ATTN_W)


def mixer(h, lp, rope, ctx_k, ctx_v, s0):
    w_in, conv_w, a_log, dt_bias, gdn_norm_w, q_norm_w, k_norm_w, w_o = lp
    bsz, t_, _ = h.shape
    proj = h @ w_in
    i1 = QKV_W
    i2 = i1 + GDN_W
    i3 = i2 + 2 * H_GDN
    i4 = i3 + 2 * H_GDN
    i5 = i4 + ATTN_W
    i6 = i5 + N_KV * HEAD_DIM
    qkv, z, b, a, q, k, v = jnp.split(proj, [i1, i2, i3, i4, i5, i6], axis=-1)
    o_gdn, s_fin = gated_deltanet(qkv, z, b, a, conv_w, a_log, dt_bias, gdn_norm_w, s0)
    q = rms_norm(q.reshape(bsz, t_, N_Q, HEAD_DIM), q_norm_w)
    k = rms_norm(k.reshape(bsz, t_, N_KV, HEAD_DIM), k_norm_w)
    v = v.reshape(bsz, t_, N_KV, HEAD_DIM)
    if rope is None:
        o_att = block_attention(q, k, v)
    else:
        q_r = apply_axial_rope(q, rope)
        k_r = apply_axial_rope(k, rope)
        k_all = jnp.concatenate([k_r, ctx_k.astype(k.dtype)], axis=1)
        v_all = jnp.concatenate([v, ctx_v.astype(v.dtype)], axis=1)
        o_att = block_attention(q_r, k_all, v_all)
    out = jnp.concatenate([o_gdn, o_att], axis=-1) @ w_o
    return out, k, v, s_fin


def trunk_layer(x, cond, lp, rope, ctx_k, ctx_v, s0):
    (w_ada, b_ada, w_in, conv_w, a_log, dt_bias, gdn_norm_w, q_norm_w, k_norm_w, w_o,
     ln1_g, ln1_b, ln2_g, ln2_b, w_gate_up, w_down) = lp
    m = jax.nn.silu(cond) @ w_ada + b_ada
    if m.ndim == 2:
        m = m[:, None, :]
    sh1, sc1, g1, sh2, sc2, g2 = jnp.split(m, 6, axis=-1)
    h = x * (1.0 + sc1) + sh1
    mix, k, v, s_fin = mixer(h, (w_in, conv_w, a_log, dt_bias, gdn_norm_w, q_norm_w, k_norm_w, w_o),
                             rope, ctx_k, ctx_v, s0)
    x = layer_norm(DN_ALPHA * x + g1 * mix, ln1_g, ln1_b)
    h = x * (1.0 + sc2) + sh2
    gate, up = jnp.split(h @ w_gate_up, 2, axis=-1)
    ffn = (jax.nn.silu(gate) * up) @ w_down
    x = layer_norm(DN_ALPHA * x + g2 * ffn, ln2_g, ln2_b)
    return x, k, v, s_fin


def setup_inputs(seed: int = 0) -> dict:
    key = jax.random.key(seed)
    ks = jax.random.split(key, 24)
    f32 = jnp.float32

    def nrm(k, shape, s):
        return s * jax.random.normal(k, shape, f32)

    dt = jnp.exp(jax.random.uniform(ks[10], (DEPTH, 2, H_GDN), f32, math.log(1e-3), math.log(1e-1)))
    return {
        'x_prompt': nrm(ks[0], (BATCH, SEQ, D_MODEL), 1.0),
        'x_sample': nrm(ks[1], (DEC_BATCH, DEC_SEQ, D_MODEL), 1.0),
        'cache_k': nrm(ks[2], (DEC_BATCH, DEPTH, PAST_LEN, N_KV, HEAD_DIM), 1.0),
        'cache_v': nrm(ks[3], (DEC_BATCH, DEPTH, PAST_LEN, N_KV, HEAD_DIM), 1.0),
        'state_gdn': nrm(ks[4], (DEC_BATCH, DEPTH, 2, H_GDN, DK_GDN, DV_GDN), 0.1),
        'c': nrm(ks[5], (DEC_BATCH, D_MODEL), 1.0),
        'c_ctx': nrm(ks[6], (D_MODEL,), 1.0),
        'w_ada': nrm(ks[7], (DEPTH, D_MODEL, 6 * D_MODEL), 0.5 * D_MODEL ** -0.5),
        'b_ada': nrm(ks[8], (DEPTH, 6 * D_MODEL), 0.01),
        'w_in': nrm(ks[9], (DEPTH, D_MODEL, IN_W), D_MODEL ** -0.5),
        'conv_w': nrm(ks[11], (DEPTH, CONV_K, QKV_W), CONV_K ** -0.5),
        'a_log': jnp.log(jax.random.uniform(ks[12], (DEPTH, 2, H_GDN), f32, 1.0, 16.0)),
        'dt_bias': dt + jnp.log(-jnp.expm1(-dt)),
        'gdn_norm_w': 1.0 + nrm(ks[13], (DEPTH, DV_GDN), 0.1),
        'q_norm_w': 1.0 + nrm(ks[14], (DEPTH, HEAD_DIM), 0.1),
        'k_norm_w': 1.0 + nrm(ks[15], (DEPTH, HEAD_DIM), 0.1),
        'w_o': nrm(ks[16], (DEPTH, MIX_W, D_MODEL), DN_BETA * MIX_W ** -0.5),
        'ln1_g': 1.0 + nrm(ks[17], (DEPTH, D_MODEL), 0.1),
        'ln1_b': nrm(ks[18], (DEPTH, D_MODEL), 0.02),
        'ln2_g': 1.0 + nrm(ks[19], (DEPTH, D_MODEL), 0.1),
        'ln2_b': nrm(ks[20], (DEPTH, D_MODEL), 0.02),
        'w_gate_up': nrm(ks[21], (DEPTH, D_MODEL, 2 * D_FF), D_MODEL ** -0.5),
        'w_down': nrm(ks[22], (DEPTH, D_FF, D_MODEL), DN_BETA * D_FF ** -0.5),
    }


def reference(x_prompt, x_sample, cache_k, cache_v, state_gdn, c, c_ctx, w_ada, b_ada, w_in,
              conv_w, a_log, dt_bias, gdn_norm_w, q_norm_w, k_norm_w, w_o, ln1_g, ln1_b,
              ln2_g, ln2_b, w_gate_up, w_down):
    rope = axial_rope_tables(x_sample.shape[1])
    s_zero = jnp.zeros((x_prompt.shape[0], 2, H_GDN, DK_GDN, DV_GDN), jnp.float32)
    xp = x_prompt
    xs = x_sample
    ks_out, vs_out, ss_out = [], [], []
    for l in range(DEPTH):
        lp = (w_ada[l], b_ada[l], w_in[l], conv_w[l], a_log[l], dt_bias[l], gdn_norm_w[l],
              q_norm_w[l], k_norm_w[l], w_o[l], ln1_g[l], ln1_b[l], ln2_g[l], ln2_b[l],
              w_gate_up[l], w_down[l])
        xp, k_ctx, v_ctx, s_ctx = trunk_layer(xp, c_ctx, lp, None, None, None, s_zero)
        ks_out.append(k_ctx)
        vs_out.append(v_ctx)
        ss_out.append(s_ctx.astype(x_prompt.dtype))
        xs, _, _, _ = trunk_layer(xs, c, lp, rope, cache_k[:, l], cache_v[:, l], state_gdn[:, l])
    new_cache_k = jnp.stack(ks_out, axis=1)
    new_cache_v = jnp.stack(vs_out, axis=1)
    new_state_gdn = jnp.stack(ss_out, axis=1)
    return (xp, xs, new_cache_k, new_cache_v, new_state_gdn)
```

```python
import os
import json
import numpy as np
import concourse.bass as bass
import concourse.mybir as mybir
from concourse.bass_utils import run_bass_kernel_spmd

F32 = mybir.dt.float32
BF16 = mybir.dt.bfloat16
AF = mybir.ActivationFunctionType
ALU = mybir.AluOpType


class _Op:
    __slots__ = ("fn", "deps", "dma", "needed", "cnt", "dsem", "dval")

    def __init__(self, fn, deps, dma):
        self.fn = fn
        self.deps = deps
        self.dma = dma
        self.needed = False
        self.cnt = 0
        self.dsem = None
        self.dval = 0


class Sched:
    ENGINES = ("pe", "act", "dve", "pool", "sp")
    NDMA = 8
    SAME_ENGINE_SYNC = ("act", "dve", "pool")

    def __init__(self, nc):
        self.nc = nc
        self.ops = {e: [] for e in self.ENGINES}
        self.res_w = {}
        self.res_r = {}
        self.nalloc = 0

    def sb(self, name, shape, dtype):
        return self.nc.alloc_sbuf_tensor(name, list(shape), dtype)

    def ps(self, name, shape, dtype=F32):
        return self.nc.alloc_psum_tensor(name, list(shape), dtype)

    def op(self, eng, fn, reads=(), writes=(), dma=False):
        ops = self.ops[eng]
        me = (eng, len(ops))
        writes = list(writes) + [k for k in reads if isinstance(k, tuple) and k and k[0] == "pb"]
        deps = set()
        for k in reads:
            w = self.res_w.get(k)
            if w is not None:
                deps.add(w)
        for k in writes:
            w = self.res_w.get(k)
            if w is not None:
                deps.add(w)
            for r in self.res_r.get(k, ()):
                deps.add(r)
        deps.discard(me)
        for k in writes:
            self.res_w[k] = me
            self.res_r[k] = []
        for k in reads:
            self.res_r.setdefault(k, []).append(me)
        ops.append(_Op(fn, deps, dma))
        return me

    def barrier(self):
        deps = set()
        for e in self.ENGINES:
            ops = self.ops[e]
            if not ops:
                continue
            for i in range(len(ops) - 1, -1, -1):
                if ops[i].fn is not None:
                    deps.add((e, i))
                    break
            nd = 0
            for i in range(len(ops) - 1, -1, -1):
                if ops[i].dma:
                    deps.add((e, i))
                    nd += 1
                    if nd >= self.NDMA:
                        break
        for e in self.ENGINES:
            self.ops[e].append(_Op(None, set(d for d in deps if not (d[0] == e and not self.ops[d[0]][d[1]].dma)), False))

    def dma(self, eng, out, in_, reads=(), writes=()):
        return self.op(eng, lambda e: e.dma_start(out=out, in_=in_), reads, writes, dma=True)

    def finish(self, final_keys=()):
        nc = self.nc
        fin = set()
        for k in final_keys:
            w = self.res_w.get(k)
            if w is not None:
                fin.add(w)
        self.ops["sp"].append(_Op(None, fin, False))
        for e in self.ENGINES:
            for o in self.ops[e]:
                for (e2, i2) in o.deps:
                    self.ops[e2][i2].needed = True
        EPOCH = 16000
        esem = {e: [nc.alloc_semaphore("s_%s%d" % (e, i)) for i in range(1 + len(self.ops[e]) // EPOCH)]
                for e in self.ENGINES}
        dsems = {e: [nc.alloc_semaphore("d_%s%d" % (e, i)) for i in range(self.NDMA)]
                 for e in self.ENGINES if any(o.dma for o in self.ops[e])}
        for e in self.ENGINES:
            c = 0
            k = 0
            for o in self.ops[e]:
                if o.dma:
                    o.dsem = dsems[e][k % self.NDMA]
                    o.dval = 16 * (k // self.NDMA + 1)
                    k += 1
                elif o.needed and o.fn is not None:
                    c += 1
                    o.dsem = esem[e][(c - 1) // EPOCH]
                    o.cnt = (c - 1) % EPOCH + 1
        handles = {"pe": "tensor", "act": "scalar", "dve": "vector", "pool": "gpsimd", "sp": "sync"}
        streams = {}
        for e in self.ENGINES:
            st = []
            waited = {}

            def wait(sem, val, st=st, waited=waited):
                key = id(sem)
                if waited.get(key, 0) >= val:
                    return
                waited[key] = val
                st.append(("w", sem, val))

            for o in self.ops[e]:
                need = {}
                for (e2, i2) in sorted(o.deps):
                    d = self.ops[e2][i2]
                    if d.dma:
                        need[id(d.dsem)] = (d.dsem, max(d.dval, need.get(id(d.dsem), (None, 0))[1]))
                    else:
                        if e2 == e and e not in self.SAME_ENGINE_SYNC:
                            continue
                        assert d.cnt > 0, (e, e2, i2)
                        need[id(d.dsem)] = (d.dsem, max(d.cnt, need.get(id(d.dsem), (None, 0))[1]))
                for sem, v in need.values():
                    wait(sem, v)
                if o.dma and o.dval > 16:
                    wait(o.dsem, o.dval - 16)
                if o.fn is None:
                    continue
                if o.dma:
                    st.append(("i", o.fn, o.dsem, 16))
                elif o.needed:
                    st.append(("i", o.fn, o.dsem, 1))
                else:
                    st.append(("i", o.fn, None, 0))
            streams[e] = st
        val = {}
        pos = {e: 0 for e in self.ENGINES}
        progress = True
        while progress:
            progress = False
            for e in self.ENGINES:
                st = streams[e]
                while pos[e] < len(st):
                    it = st[pos[e]]
                    if it[0] == "w":
                        if val.get(id(it[1]), 0) < it[2]:
                            break
                    elif it[2] is not None:
                        val[id(it[2])] = val.get(id(it[2]), 0) + it[3]
                    pos[e] += 1
                    progress = True
        stuck = {e: (pos[e], len(streams[e])) for e in self.ENGINES if pos[e] < len(streams[e])}
        assert not stuck, "deadlock in schedule: %r" % (stuck,)
        self.stats = {e: (len(streams[e]), sum(1 for it in streams[e] if it[0] == "w")) for e in self.ENGINES}
        with nc.Block() as block:
            for e in self.ENGINES:
                st = streams[e]
                if not st:
                    continue

                def body(eng, st=st):
                    for it in st:
                        if it[0] == "w":
                            eng.wait_ge(it[1], it[2])
                        else:
                            ins = it[1](eng)
                            if it[2] is not None:
                                ins.then_inc(it[2], it[3])

                getattr(block, handles[e])(body)


D = 2048
T = 1280
NSEG = 5
SEGL = 256
KC = 16
NT = 10
NCH = ((0, 512), (512, 512), (1024, 256))
DFF = 5632
IN_W = 5664
ALPHA = 4.0 ** 0.25
RMS_EPS = 1e-6
LN_EPS = 1e-5
NEG = -30000.0
C_ID, C_ONE, C_MSF, C_MIF, C_MSB, C_MIB, C_BLK, C_H0, C_H1, C_PERM = range(10)
NCONST = 10


def make_consts():
    i = np.arange(128)
    a = i[:, None]
    b = i[None, :]
    same = (a // 64) == (b // 64)
    c = np.zeros((128, NCONST, 128), np.float32)
    c[:, C_ID] = (a == b)
    c[:, C_ONE] = 1.0
    c[:, C_MSF] = same & (a < b)
    c[:, C_MIF] = same & (a <= b)
    c[:, C_MSB] = same & (a > b)
    c[:, C_MIB] = same & (a >= b)
    c[:, C_BLK] = same
    c[:, C_H0] = (a < 64) & (b >= 0)
    c[:, C_H1] = (a >= 64) & (b >= 0)
    c[:, C_PERM] = (a == (b ^ 32))
    return c


def build_program(cfg=None):
    cfg = dict(cfg or {})
    do_gdn = cfg.get("gdn", True)
    do_attn = cfg.get("attn", True)
    nlayers = cfg.get("layers", 2)
    nc = bass.Bass("TRN2", target_bir_lowering=False)

    def din(name, shape):
        return nc.dram_tensor(name, list(shape), F32, kind="ExternalInput").ap()

    def dout(name, shape):
        return nc.dram_tensor(name, list(shape), F32, kind="ExternalOutput").ap()

    xT = din("xT", [D, T])
    condT = din("condT", [128, KC, NSEG])
    w_ada = din("w_ada", [2, D, 6 * D])
    b_adaT = din("b_adaT", [128, 2, 96])
    w_in = din("w_in", [2, D, IN_W])
    convT = din("convT", [128, 2, 24, 5])
    alog_rep = din("alog_rep", [128, 2, 16])
    dtb_rep = din("dtb_rep", [128, 2, 16])
    gnw_rep = din("gnw_rep", [128, 2, 128])
    qknw = din("qknw", [128, 2, 2])
    w_o = din("w_o", [2, D, D])
    lnp_in = din("lnp", [128, 2, 4, KC])
    w_gu = din("w_gu", [2, D, 2 * DFF])
    w_dn = din("w_dn", [2, DFF, D])
    ckT = din("ckT", [2, 2, 128, 512])
    cvv = din("cvv", [2, 2, 512, 128])
    sinit = din("sinit", [2, 2, 8, 128, 128])
    ropeC_in = din("ropeC", [128, T])
    ropeS_in = din("ropeS", [128, T])
    flags_in = din("flags", [128, 2])
    consts_in = din("consts", [128, NCONST, 128])

    yT = dout("yT", [D, T])
    nk = dout("nk", [2, 2, 128, T])
    nv = dout("nv", [2, 2, 128, T])
    nst = dout("nst", [2, NSEG, 2, 8, 128, 128])

    S = Sched(nc)
    outkeys = []

    def E(eng, meth, *args, r=(), w=(), **kw):
        S.op(eng, lambda e: getattr(e, meth)(*args, **kw), r, w)

    def MM(out, lhsT, rhs, start, stop, r=(), w=()):
        S.op("pe", lambda e: e.matmul(out, lhsT, rhs, start=start, stop=stop), r, w)

    def rsqrt(out, in_, eps, scale, rk, wk):
        E("act", "activation", out, in_, AF.Ln, bias=EPS[:, {1e-6: 0, 1e-5: 1}[eps]:{1e-6: 1, 1e-5: 2}[eps]], scale=scale,
          r=[rk, "EPS"], w=[wk])
        E("act", "activation", out, out, AF.Exp, scale=-0.5, r=[wk], w=[wk])

    def TR(out, in_, ident, r=(), w=()):
        S.op("pe", lambda e: e.transpose(out, in_, ident), r, w)

    X = S.sb("X", [128, KC, T], F32)
    H = S.sb("H", [128, KC, T], BF16)
    OTG = S.sb("OTG", [128, 4, T], BF16)
    CONST = S.sb("CONST", [128, NCONST, 128], F32)
    CONSTB = S.sb("CONSTB", [128, 2, 128], BF16)
    MOD = S.sb("MOD", [128, 2, 96, NSEG], F32)
    LNP = S.sb("LNP", [128, 2, 4, KC], F32)
    LNPS = S.sb("LNPS", [128, 2, 4, KC], F32)
    FLG = S.sb("FLG", [128, 2], F32)
    CNV = S.sb("CNV", [128, 2, 24, 5], F32)
    GNW = S.sb("GNW", [128, 2, 128], F32)
    QKW = S.sb("QKW", [128, 2, 2], F32)
    NWB = 3
    WB = [S.sb("WB%d" % i, [128, KC, 128], BF16) for i in range(NWB)]
    PB = [S.ps("pb%d" % i, [128, 512], F32) for i in range(8)]
    SCR = S.sb("SCR", [128, 12288], F32)

    EPS = S.sb("EPS", [128, 4], F32)
    E("dve", "memset", EPS[:, 0:1], RMS_EPS, w=["EPS"])
    E("dve", "memset", EPS[:, 1:2], LN_EPS, w=["EPS"])
    E("dve", "memset", EPS[:, 2:3], 0.0, w=["EPS"])
    E("dve", "memset", EPS[:, 3:4], 1.0, w=["EPS"])
    ident = CONST[:, C_ID, :]
    ones_f = CONST[:, C_ONE, :]
    ident_b = CONSTB[:, 0, :]
    ones_b = CONSTB[:, 1, :]

    S.dma("sp", CONST[:], consts_in, writes=["CONST"])
    S.dma("sp", X[:], xT.rearrange("(c p) t -> p c t", p=128), writes=[("X", c) for c in range(KC)])
    S.dma("sp", LNP[:], lnp_in, writes=["LNP"])
    S.dma("sp", FLG[:], flags_in, writes=["FLG"])
    S.dma("sp", CNV[:], convT, writes=["CNV"])
    S.dma("sp", GNW[:], gnw_rep, writes=["GNW"])
    S.dma("sp", QKW[:], qknw, writes=["QKW"])
    E("dve", "tensor_copy", CONSTB[:, 0, :], ident, r=["CONST"], w=["CONSTB"])
    E("dve", "tensor_copy", CONSTB[:, 1, :], ones_f, r=["CONST"], w=["CONSTB"])
    E("dve", "tensor_scalar", LNPS[:], LNP[:], ALPHA, None, ALU.mult, r=["LNP"], w=["LNPS"])

    wb_i = [0]

    def stream_w(src_ap, nk_=KC):
        i = wb_i[0] % NWB
        wb_i[0] += 1
        S.dma("pool", WB[i][:, 0:nk_, :], src_ap.rearrange("(k p) j -> p k j", p=128),
              writes=[("WB", i)])
        return WB[i], ("WB", i)

    def scr(off, n, dtype=F32, shape=None):
        ap = SCR[:, off:off + n]
        if dtype == BF16:
            ap = SCR[:, off:off + n].bitcast(BF16)
        if shape is not None:
            ap = ap.rearrange("p (a b) -> p a b", a=shape[0])
        return ap

    CT = scr(0, KC * NSEG)
    CSt = S.sb("CSt", [128, KC * NSEG], BF16)
    BADt = S.sb("BADt", [128, 2 * 96], F32)
    CS = CSt[:]
    BAD = BADt[:]
    S.dma("sp", CT.rearrange("p (c s) -> p c s", c=KC), condT, writes=["CT"])
    S.dma("sp", BAD.rearrange("p (l j) -> p l j", l=2), b_adaT, writes=["BAD"])
    E("act", "activation", CS, CT, AF.Silu, r=["CT"], w=["CS"])
    CS3 = CS.rearrange("p (c s) -> p c s", c=KC)
    ada_next = {}

    def adaln_tiles(l, n, bank):
        j = ada_next.get(l, 0)
        j1 = min(96, j + n)
        for jj in range(j, j1):
            wt, wk = stream_w(w_ada[l, :, jj * 128:(jj + 1) * 128])
            for kc in range(KC):
                MM(PB[bank][:, jj * NSEG:(jj + 1) * NSEG], wt[:, kc, :], CS3[:, kc, :], kc == 0, kc == KC - 1,
                   r=[wk, "CS"], w=[("pb", bank)])
        ada_next[l] = j1
        if j1 == 96 and j < 96:
            E("dve", "tensor_tensor", MOD[:, l, :, :], PB[bank][:, 0:96 * NSEG].rearrange("p (j s) -> p j s", j=96),
              BAD[:, l * 96:(l + 1) * 96].unsqueeze(2).to_broadcast([128, 96, NSEG]), ALU.add,
              r=[("pb", bank), "BAD"], w=[("MOD", l)])
            for g in (1, 4):
                E("dve", "tensor_scalar", MOD[:, l, g * 16:(g + 1) * 16, :], MOD[:, l, g * 16:(g + 1) * 16, :],
                  1.0, 1.0 / ALPHA, ALU.add, ALU.mult, r=[("MOD", l)], w=[("MOD", l)])

    adaln_tiles(0, 96, 6)
    if not (do_gdn and nlayers > 1):
        for l in range(1, nlayers):
            adaln_tiles(l, 96, 6)
    S.barrier()
    for c in range(KC):
        E("act", "mul", X[:, c, :], X[:, c, :], ALPHA, r=[("X", c)], w=[("X", c)])

    def modulate(l, g_sh, g_sc):
        for c in range(KC):
            for s in range(NSEG):
                sl = slice(s * SEGL, (s + 1) * SEGL)
                E("dve", "tensor_scalar", H[:, c, sl], X[:, c, sl], MOD[:, l, g_sc * 16 + c, s:s + 1],
                  MOD[:, l, g_sh * 16 + c, s:s + 1], ALU.mult, ALU.add,
                  r=[("X", c), ("MOD", l)], w=[("H", c)])

    HK = [("H", c) for c in range(KC)]
    bank_sets = ((0, 1, 2), (3, 4, 5))
    bs_i = [0]

    def proj_rows(wt, wk, nk_=KC, rhs=None, rkeys=None):
        rhs = H if rhs is None else rhs
        rkeys = HK if rkeys is None else rkeys
        bs = bank_sets[bs_i[0] % 2]
        bs_i[0] += 1
        for kc in range(nk_):
            for n, (o, ln) in enumerate(NCH):
                MM(PB[bs[n]][:, 0:ln], wt[:, kc, :], rhs[:, kc, o:o + ln], kc == 0, kc == nk_ - 1,
                   r=[wk] + list(rkeys), w=[("pb", bs[n])])
        return bs

    def colsum_bcast(src_bf, skey, bs=None):
        bs = bank_sets[bs_i[0] % 2] if bs is None else bs
        if bs is bank_sets[bs_i[0] % 2]:
            bs_i[0] += 1
        for n, (o, ln) in enumerate(NCH):
            MM(PB[bs[n]][:, 0:ln], ones_b, src_bf[:, o:o + ln], True, True, r=["CONSTB", skey], w=[("pb", bs[n])])
        return bs

    def out_proj(l, grp):
        for m in range(KC):
            wt, wk = stream_w(w_o[l, grp * 512:(grp + 1) * 512, m * 128:(m + 1) * 128], nk_=4)
            bs = proj_rows(wt, wk, nk_=4, rhs=OTG, rkeys=["OTG"])
            for n, (o, ln) in enumerate(NCH):
                for s in range(o // SEGL, (o + ln) // SEGL):
                    so = s * SEGL - o
                    E("dve", "scalar_tensor_tensor", X[:, m, s * SEGL:(s + 1) * SEGL], PB[bs[n]][:, so:so + SEGL],
                      MOD[:, l, 2 * 16 + m, s:s + 1], X[:, m, s * SEGL:(s + 1) * SEGL], ALU.mult, ALU.add,
                      r=[("pb", bs[n]), ("MOD", l), ("X", m)], w=[("X", m)])

    def layer_norm(l, which, last):
        XB = scr(0, T // 2, BF16)
        XQ = scr(640, T // 2, BF16)
        MEAN = scr(1280, T)
        RSTD = scr(2560, T)
        NMR = scr(3840, T)
        TMP = scr(5120, T)
        sa, sb_ = bank_sets
        for c in range(KC):
            E("act", "copy", XB, X[:, c, :], r=[("X", c)], w=["XB"])
            E("dve", "tensor_tensor", XQ, X[:, c, :], X[:, c, :], ALU.mult, r=[("X", c)], w=["XQ"])
            for n, (o, ln) in enumerate(NCH):
                MM(PB[sa[n]][:, 0:ln], ones_b, XB[:, o:o + ln], c == 0, c == KC - 1, r=["CONSTB", "XB"], w=[("pb", sa[n])])
                MM(PB[sb_[n]][:, 0:ln], ones_b, XQ[:, o:o + ln], c == 0, c == KC - 1, r=["CONSTB", "XQ"], w=[("pb", sb_[n])])
        for n, (o, ln) in enumerate(NCH):
            E("act", "mul", MEAN[:, o:o + ln], PB[sa[n]][:, 0:ln], 1.0 / D, r=[("pb", sa[n])], w=["MEAN"])
            E("dve", "tensor_tensor", TMP[:, o:o + ln], MEAN[:, o:o + ln], MEAN[:, o:o + ln], ALU.mult, r=["MEAN"], w=["TMP"])
            E("dve", "scalar_tensor_tensor", RSTD[:, o:o + ln], PB[sb_[n]][:, 0:ln], 1.0 / D, TMP[:, o:o + ln],
              ALU.mult, ALU.subtract, r=[("pb", sb_[n]), "TMP"], w=["RSTD"])
        rsqrt(RSTD, RSTD, LN_EPS, 1.0, "RSTD", "RSTD")
        E("dve", "scalar_tensor_tensor", NMR, MEAN, -1.0, RSTD, ALU.mult, ALU.mult, r=["MEAN", "RSTD"], w=["NMR"])
        P_ = LNP if last else LNPS
        gi, bi = (0, 1) if which == 1 else (2, 3)
        for c in range(KC):
            E("dve", "tensor_tensor", TMP, X[:, c, :], RSTD, ALU.mult, r=[("X", c), "RSTD"], w=["TMP"])
            E("dve", "tensor_tensor", TMP, TMP, NMR, ALU.add, r=["TMP", "NMR"], w=["TMP"])
            E("act", "activation", X[:, c, :], TMP, AF.Identity, bias=P_[:, l, bi, c:c + 1], scale=P_[:, l, gi, c:c + 1],
              r=["TMP", "LNP", "LNPS"], w=[("X", c)])

    def ffn(l):
        ACTB = scr(0, 11 * T // 2, BF16, shape=(11, T))
        SG = scr(7040, T // 2, BF16)
        for grp in range(4):
            for cc in range(11):
                c = grp * 11 + cc
                wg, wgk = stream_w(w_gu[l, :, c * 128:(c + 1) * 128])
                ba = proj_rows(wg, wgk)
                wu, wuk = stream_w(w_gu[l, :, DFF + c * 128:DFF + (c + 1) * 128])
                bb = proj_rows(wu, wuk)
                for n, (o, ln) in enumerate(NCH):
                    E("act", "activation", SG[:, o:o + ln], PB[ba[n]][:, 0:ln], AF.Silu, r=[("pb", ba[n])], w=["SG"])
                    E("dve", "tensor_tensor", ACTB[:, cc, o:o + ln], SG[:, o:o + ln], PB[bb[n]][:, 0:ln], ALU.mult,
                      r=["SG", ("pb", bb[n])], w=[("ACTB", cc)])
            for m in range(KC):
                wd, wdk = stream_w(w_dn[l, grp * 1408:(grp + 1) * 1408, m * 128:(m + 1) * 128], nk_=11)
                for n, (o, ln) in enumerate(NCH):
                    pb = 6 + (m * 3 + n) % 2
                    for cc in range(11):
                        MM(PB[pb][:, 0:ln], wd[:, cc, :], ACTB[:, cc, o:o + ln], cc == 0, cc == 10,
                           r=[wdk, ("ACTB", cc)], w=[("pb", pb)])
                    for s in range(o // SEGL, (o + ln) // SEGL):
                        so = s * SEGL - o
                        E("dve", "scalar_tensor_tensor", X[:, m, s * SEGL:(s + 1) * SEGL], PB[pb][:, so:so + SEGL],
                          MOD[:, l, 5 * 16 + m, s:s + 1], X[:, m, s * SEGL:(s + 1) * SEGL], ALU.mult, ALU.add,
                          r=[("pb", pb), ("MOD", l), ("X", m)], w=[("X", m)])

    ALG = S.sb("ALG", [128, 2, 16], F32)
    DTB = S.sb("DTB", [128, 2, 16], F32)
    SSQ = S.sb("SSQ", [128, 2, NT], F32)
    S.dma("sp", ALG[:], alog_rep, writes=["ALG"])
    S.dma("sp", DTB[:], dtb_rep, writes=["DTB"])
    E("act", "activation", ALG[:], ALG[:], AF.Exp, r=["ALG"], w=["ALG"])
    E("dve", "tensor_scalar", ALG[:], ALG[:], -1.0, None, ALU.mult, r=["ALG"], w=["ALG"])
    qs_i = [0]

    def qslot():
        b = 3 + qs_i[0] % 4
        qs_i[0] += 1
        return PB[b][:, 0:128], ("pb", b)

    def gdn_layer(l):
        S.barrier()
        GP = 9620
        NBt, GCt, EGt, EDt, EGLA, EGLB = [scr(GP + 160 * i, 160) for i in range(6)]
        GT0, GT1, GT2 = [scr(10580 + 160 * i, 160) for i in range(3)]
        v3 = lambda ap: ap.rearrange("p (a b) -> p a b", a=16)
        wi = wb_i[0] % NWB
        wb_i[0] += 1
        S.dma("pool", WB[wi][:, :, 0:32], w_in[l, :, 4096:4128].rearrange("(k p) j -> p k j", p=128), writes=[("WB", wi)])
        for t in range(NT):
            for kc in range(KC):
                MM(PB[7][:, t * 32:(t + 1) * 32], H[:, kc, t * 128:(t + 1) * 128], WB[wi][:, kc, 0:32], kc == 0, kc == KC - 1,
                   r=[("WB", wi)] + HK, w=[("pb", 7)])
        pv = PB[7][:, 0:320].rearrange("p (t j) -> p t j", t=NT)
        bview = pv[:, :, 0:16].transpose([0, 2, 1])
        aview = pv[:, :, 16:32].transpose([0, 2, 1])
        E("act", "activation", v3(GT0), bview, AF.Sigmoid, r=[("pb", 7)], w=["GT0"])
        E("dve", "tensor_scalar", NBt, GT0, -1.0, None, ALU.mult, r=["GT0"], w=["NB"])
        E("dve", "tensor_tensor", v3(GT1), aview, DTB[:, l, :].unsqueeze(2).to_broadcast([128, 16, NT]), ALU.add,
          r=[("pb", 7), "DTB"], w=["GT1"])
        E("act", "activation", GT1, GT1, AF.Exp, r=["GT1"], w=["GT1"])
        E("act", "activation", GT1, GT1, AF.Ln, bias=EPS[:, 3:4], r=["GT1", "EPS"], w=["GT1"])
        E("dve", "tensor_tensor", v3(GT2), v3(GT1), ALG[:, l, :].unsqueeze(2).to_broadcast([128, 16, NT]), ALU.mult,
          r=["GT1", "ALG"], w=["GT2"])
        MM(PB[6][:, 0:80], CONST[:, C_MIF, :], GT2[:, 0:80], True, True, r=["CONST", "GT2"], w=[("pb", 6)])
        MM(PB[6][:, 80:160], CONST[:, C_MIB, :], GT2[:, 80:160], True, True, r=["CONST", "GT2"], w=[("pb", 6)])
        MM(PB[6][:, 160:320], CONST[:, C_BLK, :], GT2, True, True, r=["CONST", "GT2"], w=[("pb", 6)])
        MM(PB[6][:, 320:480], CONST[:, C_H0, :], GT2, True, True, r=["CONST", "GT2"], w=[("pb", 6)])
        MM(PB[5][:, 0:160], CONST[:, C_H1, :], GT2, True, True, r=["CONST", "GT2"], w=[("pb", 5)])
        E("dve", "tensor_copy", GCt, PB[6][:, 0:160], r=[("pb", 6)], w=["GC"])
        E("act", "activation", EGt, PB[6][:, 0:160], AF.Exp, r=[("pb", 6)], w=["EG"])
        E("dve", "tensor_tensor", GT0, PB[6][:, 160:320], GCt, ALU.subtract, r=[("pb", 6), "GC", "GT0"], w=["GT0"])
        E("act", "activation", EDt, GT0, AF.Exp, r=["GT0"], w=["ED"])
        E("act", "activation", EGLA, PB[6][:, 320:480], AF.Exp, r=[("pb", 6)], w=["EGL"])
        E("act", "activation", EGLB, PB[5][:, 0:160], AF.Exp, r=[("pb", 5)], w=["EGL"])
        S.barrier()
        for h in range(8):
            gdn_head(l, h, NBt, GCt, EGt, EDt, EGLA, EGLB)
            if h % 4 == 3:
                out_proj(l, h // 4)
        S.barrier()

    def gdn_head(l, h, NBt, GCt, EGt, EDt, EGLA, EGLB):
        CP = scr(0, 1300)
        CP3 = CP.rearrange("p (s w) -> p s w", s=NSEG)
        ACC = scr(1300, T)
        ACC3 = ACC.rearrange("p (s w) -> p s w", s=NSEG)
        SIL = scr(2580, T)
        SQB = scr(3860, T // 2, BF16)
        RN = scr(4500, T)
        QN = scr(5780, T // 2, BF16)
        KN = scr(6420, T // 2, BF16)
        VTM = scr(7060, T, shape=(NT, 128))
        KDF = scr(8340, T // 2, BF16, shape=(NT, 128))
        KDB = scr(8980, T // 2, BF16, shape=(NT, 128))
        R5 = scr(0, T, BF16, shape=(20, 128))
        ATT = scr(1280, T, BF16, shape=(20, 128))
        OTM = scr(2560, T, shape=(NT, 128))
        ZS = scr(3840, T, shape=(NT, 128))
        DIAG, EDM, TMPM, P0, P0T, P1, P1T, RR = [scr(11264 + 128 * i, 128) for i in range(8)]
        NR = scr(5120, 64, BF16)
        VN = scr(5184, 64, BF16)
        TMPS = scr(5248, 128)
        SF = [scr(5376, 128), scr(5504, 128)]
        SBF = [scr(5632, 64, BF16), scr(5696, 64, BF16)]
        S.barrier()
        for si, (col, ci) in enumerate(((h * 128, h), (1024 + h * 128, 8 + h), (2048 + h * 128, 16 + h))):
            wt, wk = stream_w(w_in[l, :, col:col + 128])
            bs = proj_rows(wt, wk)
            E("dve", "memset", CP, 0.0, w=["CP"])
            for n, (o, ln) in enumerate(NCH):
                for s in range(o // SEGL, (o + ln) // SEGL):
                    so = s * SEGL - o
                    E("act", "copy", CP3[:, s, 2:258], PB[bs[n]][:, so:so + SEGL], r=[("pb", bs[n])], w=["CP"])
            E("dve", "tensor_scalar", CP3[:, 2:5, 0:2], CP3[:, 1:4, 256:258], FLG[:, 0:1], None, ALU.mult, r=["CP", "FLG"], w=["CP"])
            E("dve", "tensor_scalar", CP3[:, 1:4, 258:260], CP3[:, 2:5, 2:4], FLG[:, 0:1], None, ALU.mult, r=["CP", "FLG"], w=["CP"])
            E("dve", "tensor_scalar", ACC3, CP3[:, :, 0:256], CNV[:, l, ci, 0:1], None, ALU.mult, r=["CP", "CNV"], w=["ACC"])
            for j in range(1, 5):
                E("dve", "scalar_tensor_tensor", ACC3, CP3[:, :, j:j + 256], CNV[:, l, ci, j:j + 1], ACC3, ALU.mult, ALU.add,
                  r=["CP", "CNV", "ACC"], w=["ACC"])
            E("act", "activation", SIL, ACC, AF.Silu, r=["ACC"], w=["SIL"])
            if si < 2:
                E("dve", "tensor_tensor", SQB, SIL, SIL, ALU.mult, r=["SIL"], w=["SQB"])
                b2 = colsum_bcast(SQB, "SQB")
                for n, (o, ln) in enumerate(NCH):
                    rsqrt(RN[:, o:o + ln], PB[b2[n]][:, 0:ln], RMS_EPS, 1.0, ("pb", b2[n]), "RN")
            if si == 0:
                E("dve", "scalar_tensor_tensor", QN, SIL, 128.0 ** -0.5, RN, ALU.mult, ALU.mult, r=["SIL", "RN"], w=["QN"])
            elif si == 1:
                E("dve", "tensor_tensor", SIL, SIL, RN, ALU.mult, r=["SIL", "RN"], w=["SIL"])
                E("act", "copy", KN, SIL, r=["SIL"], w=["KN"])
                for t in range(NT):
                    q, qk = qslot()
                    TR(q, SIL[:, t * 128:(t + 1) * 128], ident, r=["SIL", "CONST"], w=[qk])
                    cf, cb = h * NT + t, (8 + h) * NT + t
                    E("dve", "tensor_scalar", KDF[:, t, :], q, EDt[:, cf:cf + 1], None, ALU.mult, r=[qk, "ED"], w=["KDF"])
                    E("dve", "tensor_scalar", KDB[:, t, :], q, EDt[:, cb:cb + 1], None, ALU.mult, r=[qk, "ED"], w=["KDB"])
            else:
                for t in range(NT):
                    q, qk = qslot()
                    TR(q, SIL[:, t * 128:(t + 1) * 128], ident, r=["SIL", "CONST"], w=[qk])
                    E("act", "copy", VTM[:, t, :], q, r=[qk], w=["VTM"])
        S.barrier()
        rot = [0]

        def rslot():
            bnk = 2 + rot[0] % 5
            rot[0] += 1
            return PB[bnk][:, 0:128], ("pb", bnk)

        ibase = [2560, 3200, 3840, 4480, 5120, 11264]

        def inst_gen(t, d, ib, gq, kq, gk):
            Pa, Pb_, Pc, Pd, RR = [scr(ibase[ib] + 128 * i, 128) for i in range(5)]
            ka, kb_, kc_, kd, kr = [("I", ib, i) for i in range(5)]
            col = (d * 8 + h) * NT + t
            mS = CONST[:, C_MSF if d == 0 else C_MSB, :]
            mI = CONST[:, C_MIF if d == 0 else C_MIB, :]
            E("dve", "tensor_scalar", Pc, ident, GCt[:, col:col + 1], None, ALU.mult, r=["CONST", "GC"], w=[kc_])
            rq, rk = rslot()
            MM(rq, ones_f, Pc, True, True, r=["CONST", kc_], w=[rk])
            E("dve", "tensor_scalar", Pd, rq, GCt[:, col:col + 1], 0.0, ALU.subtract, ALU.min, r=[rk, "GC"], w=[kd])
            yield
            E("act", "activation", Pd, Pd, AF.Exp, r=[kd], w=[kd])
            E("dve", "tensor_tensor", Pd, Pd, mI, ALU.mult, r=[kd, "CONST"], w=[kd])
            yield
            E("dve", "scalar_tensor_tensor", Pa, gq, NBt[:, col:col + 1], Pd, ALU.mult, ALU.mult, r=[gk, "NB", kd], w=[ka])
            E("dve", "tensor_tensor", ATT[:, d * NT + t, :], kq, Pd, ALU.mult, r=[gk, kd], w=[("ATT", d, t)])
            E("dve", "tensor_tensor", Pa, Pa, mS, ALU.mult, r=[ka, "CONST"], w=[ka])
            yield
            tq, tk = rslot()
            TR(tq, Pa, ident, r=[ka, "CONST"], w=[tk])
            E("act", "copy", Pb_, tq, r=[tk], w=[kb_])
            E("dve", "tensor_tensor", RR, Pa, ident, ALU.add, r=[ka, "CONST"], w=[kr])
            yield
            st = {"cur": (Pa, ka), "curT": (Pb_, kb_), "nxt": (Pc, kc_), "nxtT": (Pd, kd)}

            def square(n):
                cur, curT, nxt, nxtT = st["cur"], st["curT"], st["nxt"], st["nxtT"]
                if n < 5:
                    aq, ak = rslot()
                    MM(aq, curT[0], cur[0], True, True, r=[curT[1], cur[1]], w=[ak])
                    E("act", "copy", nxt[0], aq, r=[ak], w=[nxt[1]])
                bq, bk = rslot()
                MM(bq, cur[0], curT[0], True, True, r=[curT[1], cur[1]], w=[bk])
                E("dve", "tensor_copy", nxtT[0], bq, r=[bk], w=[nxtT[1]])
                st["cur"], st["curT"], st["nxt"], st["nxtT"] = nxt, nxtT, cur, curT

            def update(n):
                PT = st["curT"]
                cq, ck = rslot()
                MM(cq, PT[0], RR, True, True, r=[PT[1], kr], w=[ck])
                if n < 5:
                    E("dve", "tensor_tensor", RR, RR, cq, ALU.add, r=[kr, ck], w=[kr])
                else:
                    E("dve", "tensor_tensor", R5[:, d * NT + t, :], RR, cq, ALU.add, r=[kr, ck], w=[("R5", d, t)])

            square(1)
            yield
            for n in range(1, 6):
                update(n)
                if n < 5:
                    square(n + 1)
                yield

        for tiles in ((0, 1, 2), (3, 4, 5), (6, 7), (8, 9)):
            gens = []
            for j, t in enumerate(tiles):
                tl = slice(t * 128, (t + 1) * 128)
                gb = 0 if j < 2 else 1
                gq = PB[gb][:, (2 * (j % 2)) * 128:(2 * (j % 2) + 1) * 128]
                kq = PB[gb][:, (2 * (j % 2) + 1) * 128:(2 * (j % 2) + 2) * 128]
                MM(gq, KN[:, tl], KN[:, tl], True, True, r=["KN"], w=[("pb", gb)])
                MM(kq, KN[:, tl], QN[:, tl], True, True, r=["KN", "QN"], w=[("pb", gb)])
                for d in range(2):
                    gens.append(inst_gen(t, d, 2 * j + d, gq, kq, ("pb", gb)))
            while gens:
                for g_ in list(gens):
                    try:
                        next(g_)
                    except StopIteration:
                        gens.remove(g_)
        S.barrier()
        NRs = [NR, scr(11264, 64, BF16)]
        VNs = [VN, scr(11328, 64, BF16)]
        TMPs = [TMPS, scr(11392, 128)]
        for d in range(2):
            E("dve", "memset", NRs[d], 0.0, w=[("NR", d)])
            E("dve", "memset", VNs[d], 0.0, w=[("VN", d)])
        E("dve", "memset", OTM.rearrange("p a b -> p (a b)"), 0.0, w=[("OTM", t) for t in range(NT)])
        crot = [0]

        def cslot():
            bnk = crot[0] % 7
            crot[0] += 1
            return PB[bnk][:, 0:128], ("pb", bnk)

        order = [[(t, e) for t in range(NT) for e in (0, 1)], [(t, e) for t in range(NT - 1, -1, -1) for e in (1, 0)]]
        EGLx = (EGLA, EGLB)
        def chain_step(k, d):
            t, e = order[d][k]
            s = t // 2
            tl = slice(t * 128, (t + 1) * 128)
            rows = slice(e * 64, (e + 1) * 64)
            col = (d * 8 + h) * NT + t
            SK, SBK, NK, VK, TK = ("SF", d), ("SBF", d), ("NR", d), ("VN", d), ("TMPS", d)
            NR_, VN_, TM_ = NRs[d], VNs[d], TMPs[d]
            seg_start = (t % 2 == 0 and e == 0) if d == 0 else (t % 2 == 1 and e == 1)
            seg_end = (t % 2 == 1 and e == 1) if d == 0 else (t % 2 == 0 and e == 0)
            if seg_start:
                if s == 0:
                    E("dve", "memset", SF[d], 0.0, w=[SK])
                elif (d == 0 and s == 1) or (d == 1 and s == 4):
                    S.dma("sp", SF[d], sinit[l, d, h], writes=[SK])
                else:
                    E("dve", "tensor_scalar", SF[d], SF[d], FLG[:, 0:1], None, ALU.mult, r=[SK, "FLG"], w=[SK])
                E("act", "copy", SBF[d], SF[d], r=[SK], w=[SBK])
            q1, k1 = cslot()
            MM(q1, KN[:, tl], SBF[d], True, True, r=["KN", SBK], w=[k1])
            E("dve", "scalar_tensor_tensor", NR_[rows, :], q1[rows, :], EGt[rows, col:col + 1], VTM[rows, t, :], ALU.mult, ALU.subtract,
              r=[k1, "EG", "VTM"], w=[NK])
            q3, k3 = cslot()
            MM(q3, QN[:, tl], SBF[d], True, True, r=["QN", SBK], w=[k3])
            E("act", "activation", TM_[rows, :], q3[rows, :], AF.Identity, scale=EGt[rows, col:col + 1], r=[k3, "EG"], w=[TK])
            yield
            q2, k2 = cslot()
            MM(q2, R5[:, d * NT + t, :], NR_, True, True, r=[("R5", d, t), NK], w=[k2])
            E("dve", "tensor_scalar", VN_[rows, :], q2[rows, :], NBt[rows, col:col + 1], None, ALU.mult, r=[k2, "NB"], w=[VK])
            yield
            q5, k5 = cslot()
            KD = KDF if d == 0 else KDB
            MM(q5, KD[rows, t, :], VN_[rows, :], True, True, r=["KDF", "KDB", VK], w=[k5])
            E("dve", "scalar_tensor_tensor", SBF[d], SF[d], EGLx[e][:, col:col + 1], q5, ALU.mult, ALU.add, r=[SK, "EGL", k5], w=[SBK])
            E("dve", "scalar_tensor_tensor", SF[d], SF[d], EGLx[e][:, col:col + 1], q5, ALU.mult, ALU.add, r=[SK, "EGL", k5], w=[SK])
            yield
            q4, k4 = cslot()
            MM(q4, ATT[:, d * NT + t, :], VN_, True, True, r=[("ATT", d, t), VK], w=[k4])
            E("dve", "tensor_tensor", TM_[rows, :], TM_[rows, :], q4[rows, :], ALU.add, r=[TK, k4], w=[TK])
            E("dve", "tensor_tensor", OTM[rows, t, :], OTM[rows, t, :], TM_[rows, :], ALU.add, r=[TK, ("OTM", t)], w=[("OTM", t)])
            if seg_end:
                S.dma("sp", nst[l, s, d, h], SF[d], reads=[SK], writes=[("nst", l, s, d, h)])
                outkeys.append(("nst", l, s, d, h))

        for k in range(2 * NT):
            if l + 1 < nlayers:
                adaln_tiles(l + 1, 1, 7)
            gens = [chain_step(k, 0), chain_step(k, 1)]
            while gens:
                for g_ in list(gens):
                    try:
                        next(g_)
                    except StopIteration:
                        gens.remove(g_)
        S.barrier()
        wt, wk = stream_w(w_in[l, :, 3072 + h * 128:3072 + (h + 1) * 128])
        for t in range(NT):
            zq, zk = qslot()
            for kc in range(KC):
                MM(zq, H[:, kc, t * 128:(t + 1) * 128], wt[:, kc, :], kc == 0, kc == KC - 1, r=[wk] + HK, w=[zk])
            E("act", "activation", ZS[:, t, :], zq, AF.Silu, r=[zk], w=["ZS"])
            E("act", "activation", TMPS, OTM[:, t, :], AF.Square, accum_out=SSQ[:, 0, t:t + 1], r=[("OTM", t)], w=["TMPS", "SSQ"])
        rsqrt(SSQ[:, 1, :], SSQ[:, 0, :], RMS_EPS, 1.0 / 128, "SSQ", "SSQ")
        OK_ = [("OTM", t) for t in range(NT)]
        E("dve", "tensor_tensor", OTM, OTM, SSQ[:, 1, :].unsqueeze(2).to_broadcast([128, NT, 128]), ALU.mult, r=OK_ + ["SSQ"], w=OK_)
        E("dve", "tensor_tensor", OTM, OTM, GNW[:, l, :].unsqueeze(1).to_broadcast([128, NT, 128]), ALU.mult, r=OK_ + ["GNW"], w=OK_)
        E("dve", "tensor_tensor", OTM, OTM, ZS, ALU.mult, r=OK_ + ["ZS"], w=OK_)
        for t in range(NT):
            oq, ok = qslot()
            TR(oq, OTM[:, t, :], ident, r=[("OTM", t), "CONST"], w=[ok])
            E("act", "copy", OTG[:, h % 4, t * 128:(t + 1) * 128], oq, r=[ok], w=["OTG"])

    def qk_prep(l, col, which, raw, sqb, rn, nrm, outb, ropec, ropes):
        wt, wk = stream_w(w_in[l, :, col:col + 128])
        bs = proj_rows(wt, wk)
        for n, (o, ln) in enumerate(NCH):
            E("act", "copy", raw[:, o:o + ln], PB[bs[n]][:, 0:ln], r=[("pb", bs[n])], w=["A_raw"])
        E("dve", "tensor_tensor", sqb, raw, raw, ALU.mult, r=["A_raw"], w=["A_sq"])
        b2 = colsum_bcast(sqb, "A_sq")
        for n, (o, ln) in enumerate(NCH):
            rsqrt(rn[:, o:o + ln], PB[b2[n]][:, 0:ln], RMS_EPS, 1.0 / 128, ("pb", b2[n]), "A_rn")
        E("dve", "scalar_tensor_tensor", nrm, raw, QKW[:, l, which:which + 1], rn, ALU.mult, ALU.mult,
          r=["A_raw", "QKW", "A_rn"], w=["A_nrm"])
        b3 = bank_sets[bs_i[0] % 2]
        bs_i[0] += 1
        for n, (o, ln) in enumerate(NCH):
            MM(PB[b3[n]][:, 0:ln], CONST[:, C_PERM, :], nrm[:, o:o + ln], True, True, r=["CONST", "A_nrm"], w=[("pb", b3[n])])
        E("dve", "tensor_tensor", raw, nrm, ropec, ALU.mult, r=["A_nrm", "ROPE"], w=["A_raw"])
        for n, (o, ln) in enumerate(NCH):
            E("dve", "tensor_tensor", rn[:, o:o + ln], PB[b3[n]][:, 0:ln], ropes[:, o:o + ln], ALU.mult,
              r=[("pb", b3[n]), "ROPE"], w=["A_rn"])
        E("dve", "tensor_tensor", outb, raw, rn, ALU.add, r=["A_raw", "A_rn"], w=[("A_out", id(outb))])

    def attn_layer(l):
        S.barrier()
        ROPEC = scr(0, T)
        ROPES = scr(1280, T)
        RAW = scr(2560, T)
        SQB = scr(3840, T // 2, BF16)
        RN = scr(4480, T)
        NRM = scr(5760, T)
        KRB = scr(7040, T // 2, BF16)
        VTM = scr(7680, T // 2, BF16, shape=(NT, 128))
        CKB = scr(8320, 256, BF16)
        CVB = scr(8576, 256, BF16, shape=(4, 128))
        QRB = scr(8832, T // 2, BF16)
        VA = scr(9472, T)
        PT = [scr(10752 + 256 * i, 256, BF16) for i in range(4)]
        S.dma("sp", ROPEC, ropeC_in, writes=["ROPE"])
        S.dma("sp", ROPES, ropeS_in, writes=["ROPE"])
        sc = 128.0 ** -0.5
        kq, oq = id(KRB), id(QRB)
        for g in range(2):
            qk_prep(l, 5152 + g * 128, 1, RAW, SQB, RN, NRM, KRB, ROPEC, ROPES)
            S.dma("sp", nk[l, g], NRM, reads=["A_nrm"], writes=[("nk", l, g)])
            outkeys.append(("nk", l, g))
            wt, wk = stream_w(w_in[l, :, 5408 + g * 128:5408 + (g + 1) * 128])
            bs = proj_rows(wt, wk)
            for n, (o, ln) in enumerate(NCH):
                E("act", "copy", VA[:, o:o + ln], PB[bs[n]][:, 0:ln], r=[("pb", bs[n])], w=["A_va"])
            S.dma("sp", nv[l, g], VA, reads=["A_va"], writes=[("nv", l, g)])
            outkeys.append(("nv", l, g))
            for t in range(NT):
                pb, qd = 6 + (t // 4) % 2, t % 4
                TR(PB[pb][:, qd * 128:(qd + 1) * 128], VA[:, t * 128:(t + 1) * 128], ident, r=["A_va", "CONST"], w=[("pb", pb)])
                E("act", "copy", VTM[:, t, :], PB[pb][:, qd * 128:(qd + 1) * 128], r=[("pb", pb)], w=["A_vtm"])
            S.dma("pool", CKB, ckT[l, g], writes=["A_ck"])
            S.dma("pool", CVB, cvv[l, g].rearrange("(b p) d -> p b d", p=128), writes=["A_cv"])
            for hl in range(4):
                hq = 4 * g + hl
                qk_prep(l, 4128 + hq * 128, 0, RAW, SQB, RN, NRM, QRB, ROPEC, ROPES)
                jobs = [(0, 256, [("l", kb) for kb in (0, 1)])]
                for qc in range(2):
                    jobs.append((256 + qc * 512, 512, [("l", kb) for kb in range(2, 10)] + [("c", cb) for cb in range(4)]))
                for (q0, qn, blocks) in jobs:
                    nb = len(blocks)
                    STB = (0, 1, 4, 5)

                    def emit_st(bi):
                        kind, kb = blocks[bi]
                        sl_ = bi % 4
                        pb = STB[sl_]
                        if kind == "l":
                            klhs, kr = KRB[:, kb * 128:(kb + 1) * 128], ("A_out", kq)
                        else:
                            klhs, kr = CKB[:, kb * 128:(kb + 1) * 128], "A_ck"
                        MM(PB[pb][:, 0:qn], klhs, QRB[:, q0:q0 + qn], True, True, r=[kr, ("A_out", oq)], w=[("pb", pb)])
                        for hh in range(qn // 256):
                            qs = (q0 + hh * 256) // 256
                            same = (kind == "l") and (kb // 2 == qs)
                            E("act", "activation", PT[sl_][:, hh * 256:(hh + 1) * 256], PB[pb][:, hh * 256:(hh + 1) * 256], AF.Exp,
                              bias=(EPS[:, 2:3] if same else FLG[:, 1:2]), scale=sc, r=[("pb", pb), "FLG", "EPS"], w=[("PT", sl_)])

                    def emit_pv(bi):
                        kind, kb = blocks[bi]
                        sl_ = bi % 4
                        if kind == "l":
                            vlhs, vr = VTM[:, kb, :], "A_vtm"
                        else:
                            vlhs, vr = CVB[:, kb, :], "A_cv"
                        MM(PB[2][:, 0:qn], vlhs, PT[sl_][:, 0:qn], bi == 0, bi == nb - 1, r=[vr, ("PT", sl_)], w=[("pb", 2)])
                        MM(PB[3][:, 0:qn], ones_b, PT[sl_][:, 0:qn], bi == 0, bi == nb - 1, r=["CONSTB", ("PT", sl_)], w=[("pb", 3)])

                    LA = 2
                    for bi in range(min(LA, nb)):
                        emit_st(bi)
                    for bi in range(nb):
                        if bi + LA < nb:
                            emit_st(bi + LA)
                        emit_pv(bi)
                    E("dve", "reciprocal", RAW[:, 0:qn], PB[3][:, 0:qn], r=[("pb", 3)], w=["A_raw"])
                    E("dve", "tensor_tensor", OTG[:, hl, q0:q0 + qn], PB[2][:, 0:qn], RAW[:, 0:qn], ALU.mult,
                      r=[("pb", 2), "A_raw"], w=["OTG"])
            out_proj(l, 2 + g)
        S.barrier()

    for l in range(nlayers):
        modulate(l, 0, 1)
        if do_gdn:
            gdn_layer(l)
        if do_attn:
            attn_layer(l)
        if not (do_gdn or do_attn):
            pass
        S.barrier()
        layer_norm(l, 1, False)
        modulate(l, 3, 4)
        S.barrier()
        ffn(l)
        S.barrier()
        layer_norm(l, 2, l == nlayers - 1)
        S.barrier()

    for c in range(KC):
        S.dma("sp", yT[c * 128:(c + 1) * 128, :], X[:, c, :], reads=[("X", c)], writes=[("yT", c)])
        outkeys.append(("yT", c))
    S.finish(final_keys=outkeys)
    return nc


def _core_segments(core):
    if core < 6:
        return [("p", 5 * core + s, 0) for s in range(5)]
    b = core - 6
    return [("p", 30 + b, 0)] + [("s", b, q) for q in range(4)]


def _rope_tables(core):
    C = np.ones((128, T), np.float32)
    Sg = np.zeros((128, T), np.float32)
    if core >= 6:
        pos = np.arange(1024)
        row = (pos // 64).astype(np.float32)
        col = (pos % 64).astype(np.float32)
        inv = (np.float32(10000.0) ** (-np.arange(0, 64, 2, dtype=np.float32) / np.float32(64))).astype(np.float32)
        d = np.arange(128)
        ang = np.where((d < 64)[:, None], row[None, :] * inv[d % 32][:, None], col[None, :] * inv[d % 32][:, None])
        ang = ang.astype(np.float32)
        sign = np.where((d % 64) < 32, -1.0, 1.0).astype(np.float32)[:, None]
        C[:, 256:] = np.cos(ang)
        Sg[:, 256:] = np.sin(ang) * sign
    return C, Sg


def make_in_maps(inp):
    f = np.float32
    g = lambda k: np.asarray(inp[k], dtype=f)
    x_prompt, x_sample = g("x_prompt"), g("x_sample")
    cache_k, cache_v, state_gdn = g("cache_k"), g("cache_v"), g("state_gdn")
    c, c_ctx = g("c"), g("c_ctx")
    shared = {
        "w_ada": g("w_ada"), "w_in": g("w_in"), "w_o": g("w_o"), "w_gu": g("w_gate_up"), "w_dn": g("w_down"),
        "b_adaT": np.ascontiguousarray(g("b_ada").reshape(2, 96, 128).transpose(2, 0, 1)),
        "convT": np.ascontiguousarray(g("conv_w").reshape(2, 5, 24, 128).transpose(3, 0, 2, 1)),
        "alog_rep": np.ascontiguousarray(np.broadcast_to(g("a_log").reshape(1, 2, 16), (128, 2, 16))),
        "dtb_rep": np.ascontiguousarray(np.broadcast_to(g("dt_bias").reshape(1, 2, 16), (128, 2, 16))),
        "gnw_rep": np.ascontiguousarray(np.broadcast_to(g("gdn_norm_w").reshape(1, 2, 128), (128, 2, 128))),
        "qknw": np.ascontiguousarray(np.stack([g("q_norm_w"), g("k_norm_w")], 0).transpose(2, 1, 0)),
        "lnp": np.ascontiguousarray(np.stack([g("ln1_g"), g("ln1_b"), g("ln2_g"), g("ln2_b")], 0)
                                    .reshape(4, 2, 16, 128).transpose(3, 1, 0, 2)),
        "consts": make_consts(),
    }
    maps = []
    for core in range(8):
        segs = _core_segments(core)
        xs, cs = [], []
        for kind, i, q in segs:
            if kind == "p":
                xs.append(x_prompt[i])
                cs.append(c_ctx)
            else:
                xs.append(x_sample[i, q * 256:(q + 1) * 256])
                cs.append(c[i])
        m = dict(shared)
        m["xT"] = np.ascontiguousarray(np.concatenate(xs, 0).T)
        m["condT"] = np.ascontiguousarray(np.stack(cs, 0).reshape(5, 16, 128).transpose(2, 1, 0))
        if core >= 6:
            b = core - 6
            m["ckT"] = np.ascontiguousarray(cache_k[b].transpose(0, 2, 3, 1))
            m["cvv"] = np.ascontiguousarray(cache_v[b].transpose(0, 2, 1, 3))
            m["sinit"] = np.ascontiguousarray(state_gdn[b])
            m["flags"] = np.ascontiguousarray(np.broadcast_to(np.array([1.0, 0.0], f), (128, 2)))
        else:
            m["ckT"] = np.zeros((2, 2, 128, 512), f)
            m["cvv"] = np.zeros((2, 2, 512, 128), f)
            m["sinit"] = np.zeros((2, 2, 8, 128, 128), f)
            m["flags"] = np.ascontiguousarray(np.broadcast_to(np.array([0.0, NEG], f), (128, 2)))
        m["ropeC"], m["ropeS"] = _rope_tables(core)
        maps.append(m)
    return maps


def assemble(results):
    f = np.float32
    y_p = np.zeros((32, 256, D), f)
    y_s = np.zeros((2, 1024, D), f)
    ck = np.zeros((32, 2, 256, 2, 128), f)
    cv = np.zeros((32, 2, 256, 2, 128), f)
    st = np.zeros((32, 2, 2, 8, 128, 128), f)
    for core in range(8):
        r = results[core]
        y = np.asarray(r["yT"]).T
        nk_ = np.asarray(r["nk"])
        nv_ = np.asarray(r["nv"])
        ns_ = np.asarray(r["nst"])
        for s, (kind, i, q) in enumerate(_core_segments(core)):
            sl = slice(s * 256, (s + 1) * 256)
            if kind == "p":
                y_p[i] = y[sl]
                ck[i] = nk_[:, :, :, sl].transpose(0, 3, 1, 2)
                cv[i] = nv_[:, :, :, sl].transpose(0, 3, 1, 2)
                st[i] = ns_[:, s]
            else:
                y_s[i, q * 256:(q + 1) * 256] = y[sl]
    return y_p, y_s, ck, cv, st


_NC_CACHE = {}


def kernel(**inputs):
    maps = make_in_maps(inputs)
    if "nc" not in _NC_CACHE:
        _NC_CACHE["nc"] = build_program(json.loads(os.environ.get("KCFG", "{}")))
    res = run_bass_kernel_spmd(_NC_CACHE["nc"], maps, core_ids=list(range(8)))
    return assemble(res.results)
```

```python
import os
import json
import numpy as np
import concourse.bass as bass
import concourse.mybir as mybir
from concourse.bass_utils import run_bass_kernel_spmd

F32 = mybir.dt.float32
BF16 = mybir.dt.bfloat16
AF = mybir.ActivationFunctionType
ALU = mybir.AluOpType


class _Op:
    __slots__ = ("fn", "deps", "dma", "needed", "cnt", "dsem", "dval")

    def __init__(self, fn, deps, dma):
        self.fn = fn
        self.deps = deps
        self.dma = dma
        self.needed = False
        self.cnt = 0
        self.dsem = None
        self.dval = 0


class Sched:
    ENGINES = ("pe", "act", "dve", "pool", "sp")
    NDMA = 8
    SAME_ENGINE_SYNC = ("act", "dve", "pool")

    def __init__(self, nc):
        self.nc = nc
        self.ops = {e: [] for e in self.ENGINES}
        self.res_w = {}
        self.res_r = {}
        self.nalloc = 0

    def sb(self, name, shape, dtype):
        return self.nc.alloc_sbuf_tensor(name, list(shape), dtype)

    def ps(self, name, shape, dtype=F32):
        return self.nc.alloc_psum_tensor(name, list(shape), dtype)

    def op(self, eng, fn, reads=(), writes=(), dma=False):
        ops = self.ops[eng]
        me = (eng, len(ops))
        writes = list(writes) + [k for k in reads if isinstance(k, tuple) and k and k[0] == "pb"]
        deps = set()
        for k in reads:
            w = self.res_w.get(k)
            if w is not None:
                deps.add(w)
        for k in writes:
            w = self.res_w.get(k)
            if w is not None:
                deps.add(w)
            for r in self.res_r.get(k, ()):
                deps.add(r)
        deps.discard(me)
        for k in writes:
            self.res_w[k] = me
            self.res_r[k] = []
        for k in reads:
            self.res_r.setdefault(k, []).append(me)
        ops.append(_Op(fn, deps, dma))
        return me

    def barrier(self):
        deps = set()
        for e in self.ENGINES:
            ops = self.ops[e]
            if not ops:
                continue
            for i in range(len(ops) - 1, -1, -1):
                if ops[i].fn is not None:
                    deps.add((e, i))
                    break
            nd = 0
            for i in range(len(ops) - 1, -1, -1):
                if ops[i].dma:
                    deps.add((e, i))
                    nd += 1
                    if nd >= self.NDMA:
                        break
        for e in self.ENGINES:
            self.ops[e].append(_Op(None, set(d for d in deps if not (d[0] == e and not self.ops[d[0]][d[1]].dma)), False))

    def dma(self, eng, out, in_, reads=(), writes=()):
        return self.op(eng, lambda e: e.dma_start(out=out, in_=in_), reads, writes, dma=True)

    def finish(self, final_keys=()):
        nc = self.nc
        fin = set()
        for k in final_keys:
            w = self.res_w.get(k)
            if w is not None:
                fin.add(w)
        self.ops["sp"].append(_Op(None, fin, False))
        for e in self.ENGINES:
            for o in self.ops[e]:
                for (e2, i2) in o.deps:
                    self.ops[e2][i2].needed = True
        EPOCH = 16000
        esem = {e: [nc.alloc_semaphore("s_%s%d" % (e, i)) for i in range(1 + len(self.ops[e]) // EPOCH)]
                for e in self.ENGINES}
        dsems = {e: [nc.alloc_semaphore("d_%s%d" % (e, i)) for i in range(self.NDMA)]
                 for e in self.ENGINES if any(o.dma for o in self.ops[e])}
        for e in self.ENGINES:
            c = 0
            k = 0
            for o in self.ops[e]:
                if o.dma:
                    o.dsem = dsems[e][k % self.NDMA]
                    o.dval = 16 * (k // self.NDMA + 1)
                    k += 1
                elif o.needed and o.fn is not None:
                    c += 1
                    o.dsem = esem[e][(c - 1) // EPOCH]
                    o.cnt = (c - 1) % EPOCH + 1
        handles = {"pe": "tensor", "act": "scalar", "dve": "vector", "pool": "gpsimd", "sp": "sync"}
        streams = {}
        for e in self.ENGINES:
            st = []
            waited = {}

            def wait(sem, val, st=st, waited=waited):
                key = id(sem)
                if waited.get(key, 0) >= val:
                    return
                waited[key] = val
                st.append(("w", sem, val))

            for o in self.ops[e]:
                need = {}
                for (e2, i2) in sorted(o.deps):
                    d = self.ops[e2][i2]
                    if d.dma:
                        need[id(d.dsem)] = (d.dsem, max(d.dval, need.get(id(d.dsem), (None, 0))[1]))
                    else:
                        if e2 == e and e not in self.SAME_ENGINE_SYNC:
                            continue
                        assert d.cnt > 0, (e, e2, i2)
                        need[id(d.dsem)] = (d.dsem, max(d.cnt, need.get(id(d.dsem), (None, 0))[1]))
                for sem, v in need.values():
                    wait(sem, v)
                if o.dma and o.dval > 16:
                    wait(o.dsem, o.dval - 16)
                if o.fn is None:
                    continue
                if o.dma:
                    st.append(("i", o.fn, o.dsem, 16))
                elif o.needed:
                    st.append(("i", o.fn, o.dsem, 1))
                else:
                    st.append(("i", o.fn, None, 0))
            streams[e] = st
        val = {}
        pos = {e: 0 for e in self.ENGINES}
        progress = True
        while progress:
            progress = False
            for e in self.ENGINES:
                st = streams[e]
                while pos[e] < len(st):
                    it = st[pos[e]]
                    if it[0] == "w":
                        if val.get(id(it[1]), 0) < it[2]:
                            break
                    elif it[2] is not None:
                        val[id(it[2])] = val.get(id(it[2]), 0) + it[3]
                    pos[e] += 1
                    progress = True
        stuck = {e: (pos[e], len(streams[e])) for e in self.ENGINES if pos[e] < len(streams[e])}
        assert not stuck, "deadlock in schedule: %r" % (stuck,)
        self.stats = {e: (len(streams[e]), sum(1 for it in streams[e] if it[0] == "w")) for e in self.ENGINES}
        with nc.Block() as block:
            for e in self.ENGINES:
                st = streams[e]
                if not st:
                    continue

                def body(eng, st=st):
                    for it in st:
                        if it[0] == "w":
                            eng.wait_ge(it[1], it[2])
                        else:
                            ins = it[1](eng)
                            if it[2] is not None:
                                ins.then_inc(it[2], it[3])

                getattr(block, handles[e])(body)


D = 2048
T = 1280
NSEG = 5
SEGL = 256
KC = 16
NT = 10
NCH = ((0, 512), (512, 512), (1024, 256))
DFF = 5632
IN_W = 5664
ALPHA = 4.0 ** 0.25
RMS_EPS = 1e-6
LN_EPS = 1e-5
NEG = -30000.0
C_ID, C_ONE, C_MSF, C_MIF, C_MSB, C_MIB, C_BLK, C_H0, C_H1, C_PERM = range(10)
NCONST = 10


def make_consts():
    i = np.arange(128)
    a = i[:, None]
    b = i[None, :]
    same = (a // 64) == (b // 64)
    c = np.zeros((128, NCONST, 128), np.float32)
    c[:, C_ID] = (a == b)
    c[:, C_ONE] = 1.0
    c[:, C_MSF] = same & (a < b)
    c[:, C_MIF] = same & (a <= b)
    c[:, C_MSB] = same & (a > b)
    c[:, C_MIB] = same & (a >= b)
    c[:, C_BLK] = same
    c[:, C_H0] = (a < 64) & (b >= 0)
    c[:, C_H1] = (a >= 64) & (b >= 0)
    c[:, C_PERM] = (a == (b ^ 32))
    return c


def build_program(cfg=None):
    cfg = dict(cfg or {})
    do_gdn = cfg.get("gdn", True)
    do_attn = cfg.get("attn", True)
    nlayers = cfg.get("layers", 2)
    nc = bass.Bass("TRN2", target_bir_lowering=False)

    def din(name, shape):
        return nc.dram_tensor(name, list(shape), F32, kind="ExternalInput").ap()

    def dout(name, shape):
        return nc.dram_tensor(name, list(shape), F32, kind="ExternalOutput").ap()

    xT = din("xT", [D, T])
    condT = din("condT", [128, KC, NSEG])
    w_ada = din("w_ada", [2, D, 6 * D])
    b_adaT = din("b_adaT", [128, 2, 96])
    w_in = din("w_in", [2, D, IN_W])
    convT = din("convT", [128, 2, 24, 5])
    alog_rep = din("alog_rep", [128, 2, 16])
    dtb_rep = din("dtb_rep", [128, 2, 16])
    gnw_rep = din("gnw_rep", [128, 2, 128])
    qknw = din("qknw", [128, 2, 2])
    w_o = din("w_o", [2, D, D])
    lnp_in = din("lnp", [128, 2, 4, KC])
    w_gu = din("w_gu", [2, D, 2 * DFF])
    w_dn = din("w_dn", [2, DFF, D])
    ckT = din("ckT", [2, 2, 128, 512])
    cvv = din("cvv", [2, 2, 512, 128])
    sinit = din("sinit", [2, 2, 8, 128, 128])
    ropeC_in = din("ropeC", [128, T])
    ropeS_in = din("ropeS", [128, T])
    flags_in = din("flags", [128, 2])
    consts_in = din("consts", [128, NCONST, 128])

    yT = dout("yT", [D, T])
    nk = dout("nk", [2, 2, 128, T])
    nv = dout("nv", [2, 2, 128, T])
    nst = dout("nst", [2, NSEG, 2, 8, 128, 128])

    S = Sched(nc)
    outkeys = []

    def E(eng, meth, *args, r=(), w=(), **kw):
        S.op(eng, lambda e: getattr(e, meth)(*args, **kw), r, w)

    def MM(out, lhsT, rhs, start, stop, r=(), w=()):
        S.op("pe", lambda e: e.matmul(out, lhsT, rhs, start=start, stop=stop), r, w)

    def rsqrt(out, in_, eps, scale, rk, wk):
        E("act", "activation", out, in_, AF.Ln, bias=EPS[:, {1e-6: 0, 1e-5: 1}[eps]:{1e-6: 1, 1e-5: 2}[eps]], scale=scale,
          r=[rk, "EPS"], w=[wk])
        E("act", "activation", out, out, AF.Exp, scale=-0.5, r=[wk], w=[wk])

    def TR(out, in_, ident, r=(), w=()):
        S.op("pe", lambda e: e.transpose(out, in_, ident), r, w)

    X = S.sb("X", [128, KC, T], F32)
    H = S.sb("H", [128, KC, T], BF16)
    OTG = S.sb("OTG", [128, 4, T], BF16)
    CONST = S.sb("CONST", [128, NCONST, 128], F32)
    CONSTB = S.sb("CONSTB", [128, 2, 128], BF16)
    MOD = S.sb("MOD", [128, 2, 96, NSEG], F32)
    LNP = S.sb("LNP", [128, 2, 4, KC], F32)
    LNPS = S.sb("LNPS", [128, 2, 4, KC], F32)
    FLG = S.sb("FLG", [128, 2], F32)
    CNV = S.sb("CNV", [128, 2, 24, 5], F32)
    GNW = S.sb("GNW", [128, 2, 128], F32)
    QKW = S.sb("QKW", [128, 2, 2], F32)
    NWB = 3
    WB = [S.sb("WB%d" % i, [128, KC, 128], BF16) for i in range(NWB)]
    PB = [S.ps("pb%d" % i, [128, 512], F32) for i in range(8)]
    SCR = S.sb("SCR", [128, 12288], F32)

    EPS = S.sb("EPS", [128, 4], F32)
    E("dve", "memset", EPS[:, 0:1], RMS_EPS, w=["EPS"])
    E("dve", "memset", EPS[:, 1:2], LN_EPS, w=["EPS"])
    E("dve", "memset", EPS[:, 2:3], 0.0, w=["EPS"])
    E("dve", "memset", EPS[:, 3:4], 1.0, w=["EPS"])
    ident = CONST[:, C_ID, :]
    ones_f = CONST[:, C_ONE, :]
    ident_b = CONSTB[:, 0, :]
    ones_b = CONSTB[:, 1, :]

    S.dma("sp", CONST[:], consts_in, writes=["CONST"])
    S.dma("sp", X[:], xT.rearrange("(c p) t -> p c t", p=128), writes=[("X", c) for c in range(KC)])
    S.dma("sp", LNP[:], lnp_in, writes=["LNP"])
    S.dma("sp", FLG[:], flags_in, writes=["FLG"])
    S.dma("sp", CNV[:], convT, writes=["CNV"])
    S.dma("sp", GNW[:], gnw_rep, writes=["GNW"])
    S.dma("sp", QKW[:], qknw, writes=["QKW"])
    E("dve", "tensor_copy", CONSTB[:, 0, :], ident, r=["CONST"], w=["CONSTB"])
    E("dve", "tensor_copy", CONSTB[:, 1, :], ones_f, r=["CONST"], w=["CONSTB"])
    E("dve", "tensor_scalar", LNPS[:], LNP[:], ALPHA, None, ALU.mult, r=["LNP"], w=["LNPS"])

    wb_i = [0]
    wb_pinned = set()

    def stream_w(src_ap, nk_=KC):
        i = wb_i[0] % NWB
        while i in wb_pinned:
            wb_i[0] += 1
            i = wb_i[0] % NWB
        wb_i[0] += 1
        S.dma("pool", WB[i][:, 0:nk_, :], src_ap.rearrange("(k p) j -> p k j", p=128),
              writes=[("WB", i)])
        return WB[i], ("WB", i)

    def scr(off, n, dtype=F32, shape=None):
        ap = SCR[:, off:off + n]
        if dtype == BF16:
            ap = SCR[:, off:off + n].bitcast(BF16)
        if shape is not None:
            ap = ap.rearrange("p (a b) -> p a b", a=shape[0])
        return ap

    CT = scr(0, KC * NSEG)
    CSt = S.sb("CSt", [128, KC * NSEG], BF16)
    BADt = S.sb("BADt", [128, 2 * 96], F32)
    CS = CSt[:]
    BAD = BADt[:]
    S.dma("sp", CT.rearrange("p (c s) -> p c s", c=KC), condT, writes=["CT"])
    S.dma("sp", BAD.rearrange("p (l j) -> p l j", l=2), b_adaT, writes=["BAD"])
    E("act", "activation", CS, CT, AF.Silu, r=["CT"], w=["CS"])
    CS3 = CS.rearrange("p (c s) -> p c s", c=KC)
    ada_next = {}

    def adaln_tiles(l, n, bank):
        j = ada_next.get(l, 0)
        j1 = min(96, j + n)
        for jj in range(j, j1):
            wt, wk = stream_w(w_ada[l, :, jj * 128:(jj + 1) * 128])
            for kc in range(KC):
                MM(PB[bank][:, jj * NSEG:(jj + 1) * NSEG], wt[:, kc, :], CS3[:, kc, :], kc == 0, kc == KC - 1,
                   r=[wk, "CS"], w=[("pb", bank)])
        ada_next[l] = j1
        if j1 == 96 and j < 96:
            E("dve", "tensor_tensor", MOD[:, l, :, :], PB[bank][:, 0:96 * NSEG].rearrange("p (j s) -> p j s", j=96),
              BAD[:, l * 96:(l + 1) * 96].unsqueeze(2).to_broadcast([128, 96, NSEG]), ALU.add,
              r=[("pb", bank), "BAD"], w=[("MOD", l)])
            for g in (1, 4):
                E("dve", "tensor_scalar", MOD[:, l, g * 16:(g + 1) * 16, :], MOD[:, l, g * 16:(g + 1) * 16, :],
                  1.0, 1.0 / ALPHA, ALU.add, ALU.mult, r=[("MOD", l)], w=[("MOD", l)])

    adaln_tiles(0, 96, 6)
    if not (do_gdn and nlayers > 1):
        for l in range(1, nlayers):
            adaln_tiles(l, 96, 6)
    S.barrier()
    for c in range(KC):
        E("act", "mul", X[:, c, :], X[:, c, :], ALPHA, r=[("X", c)], w=[("X", c)])

    def modulate(l, g_sh, g_sc):
        for c in range(KC):
            for s in range(NSEG):
                sl = slice(s * SEGL, (s + 1) * SEGL)
                E("dve", "tensor_scalar", H[:, c, sl], X[:, c, sl], MOD[:, l, g_sc * 16 + c, s:s + 1],
                  MOD[:, l, g_sh * 16 + c, s:s + 1], ALU.mult, ALU.add,
                  r=[("X", c), ("MOD", l)], w=[("H", c)])

    HK = [("H", c) for c in range(KC)]
    bank_sets = ((0, 1, 2), (3, 4, 5))
    bs_i = [0]

    def proj_rows(wt, wk, nk_=KC, rhs=None, rkeys=None, bs=None):
        rhs = H if rhs is None else rhs
        rkeys = HK if rkeys is None else rkeys
        if bs is None:
            bs = bank_sets[bs_i[0] % 2]
            bs_i[0] += 1
        for kc in range(nk_):
            for n, (o, ln) in enumerate(NCH):
                MM(PB[bs[n]][:, 0:ln], wt[:, kc, :], rhs[:, kc, o:o + ln], kc == 0, kc == nk_ - 1,
                   r=[wk] + list(rkeys), w=[("pb", bs[n])])
        return bs

    def colsum_bcast(src_bf, skey, bs=None):
        if bs is None:
            bs = bank_sets[bs_i[0] % 2]
            bs_i[0] += 1
        for n, (o, ln) in enumerate(NCH):
            MM(PB[bs[n]][:, 0:ln], ones_b, src_bf[:, o:o + ln], True, True, r=["CONSTB", skey], w=[("pb", bs[n])])
        return bs

    def out_proj(l, grp):
        for m in range(KC):
            wt, wk = stream_w(w_o[l, grp * 512:(grp + 1) * 512, m * 128:(m + 1) * 128], nk_=4)
            bs = proj_rows(wt, wk, nk_=4, rhs=OTG, rkeys=["OTG"])
            for n, (o, ln) in enumerate(NCH):
                for s in range(o // SEGL, (o + ln) // SEGL):
                    so = s * SEGL - o
                    E("dve", "scalar_tensor_tensor", X[:, m, s * SEGL:(s + 1) * SEGL], PB[bs[n]][:, so:so + SEGL],
                      MOD[:, l, 2 * 16 + m, s:s + 1], X[:, m, s * SEGL:(s + 1) * SEGL], ALU.mult, ALU.add,
                      r=[("pb", bs[n]), ("MOD", l), ("X", m)], w=[("X", m)])

    def layer_norm(l, which, last):
        XB = scr(0, T // 2, BF16)
        XQ = scr(640, T // 2, BF16)
        MEAN = scr(1280, T)
        RSTD = scr(2560, T)
        NMR = scr(3840, T)
        TMP = scr(5120, T)
        sa, sb_ = bank_sets
        for c in range(KC):
            E("act", "copy", XB, X[:, c, :], r=[("X", c)], w=["XB"])
            E("dve", "tensor_tensor", XQ, X[:, c, :], X[:, c, :], ALU.mult, r=[("X", c)], w=["XQ"])
            for n, (o, ln) in enumerate(NCH):
                MM(PB[sa[n]][:, 0:ln], ones_b, XB[:, o:o + ln], c == 0, c == KC - 1, r=["CONSTB", "XB"], w=[("pb", sa[n])])
                MM(PB[sb_[n]][:, 0:ln], ones_b, XQ[:, o:o + ln], c == 0, c == KC - 1, r=["CONSTB", "XQ"], w=[("pb", sb_[n])])
        for n, (o, ln) in enumerate(NCH):
            E("act", "mul", MEAN[:, o:o + ln], PB[sa[n]][:, 0:ln], 1.0 / D, r=[("pb", sa[n])], w=["MEAN"])
            E("dve", "tensor_tensor", TMP[:, o:o + ln], MEAN[:, o:o + ln], MEAN[:, o:o + ln], ALU.mult, r=["MEAN"], w=["TMP"])
            E("dve", "scalar_tensor_tensor", RSTD[:, o:o + ln], PB[sb_[n]][:, 0:ln], 1.0 / D, TMP[:, o:o + ln],
              ALU.mult, ALU.subtract, r=[("pb", sb_[n]), "TMP"], w=["RSTD"])
        rsqrt(RSTD, RSTD, LN_EPS, 1.0, "RSTD", "RSTD")
        E("dve", "scalar_tensor_tensor", NMR, MEAN, -1.0, RSTD, ALU.mult, ALU.mult, r=["MEAN", "RSTD"], w=["NMR"])
        P_ = LNP if last else LNPS
        gi, bi = (0, 1) if which == 1 else (2, 3)
        for c in range(KC):
            E("dve", "tensor_tensor", TMP, X[:, c, :], RSTD, ALU.mult, r=[("X", c), "RSTD"], w=["TMP"])
            E("dve", "tensor_tensor", TMP, TMP, NMR, ALU.add, r=["TMP", "NMR"], w=["TMP"])
            E("act", "activation", X[:, c, :], TMP, AF.Identity, bias=P_[:, l, bi, c:c + 1], scale=P_[:, l, gi, c:c + 1],
              r=["TMP", "LNP", "LNPS"], w=[("X", c)])

    def ffn(l):
        ACTB = scr(0, 11 * T // 2, BF16, shape=(11, T))
        SG = scr(7040, T // 2, BF16)
        for grp in range(4):
            for cc in range(11):
                c = grp * 11 + cc
                wg, wgk = stream_w(w_gu[l, :, c * 128:(c + 1) * 128])
                ba = proj_rows(wg, wgk)
                wu, wuk = stream_w(w_gu[l, :, DFF + c * 128:DFF + (c + 1) * 128])
                bb = proj_rows(wu, wuk)
                for n, (o, ln) in enumerate(NCH):
                    E("act", "activation", SG[:, o:o + ln], PB[ba[n]][:, 0:ln], AF.Silu, r=[("pb", ba[n])], w=["SG"])
                    E("dve", "tensor_tensor", ACTB[:, cc, o:o + ln], SG[:, o:o + ln], PB[bb[n]][:, 0:ln], ALU.mult,
                      r=["SG", ("pb", bb[n])], w=[("ACTB", cc)])
            for m in range(KC):
                wd, wdk = stream_w(w_dn[l, grp * 1408:(grp + 1) * 1408, m * 128:(m + 1) * 128], nk_=11)
                for n, (o, ln) in enumerate(NCH):
                    pb = 6 + (m * 3 + n) % 2
                    for cc in range(11):
                        MM(PB[pb][:, 0:ln], wd[:, cc, :], ACTB[:, cc, o:o + ln], cc == 0, cc == 10,
                           r=[wdk, ("ACTB", cc)], w=[("pb", pb)])
                    for s in range(o // SEGL, (o + ln) // SEGL):
                        so = s * SEGL - o
                        E("dve", "scalar_tensor_tensor", X[:, m, s * SEGL:(s + 1) * SEGL], PB[pb][:, so:so + SEGL],
                          MOD[:, l, 5 * 16 + m, s:s + 1], X[:, m, s * SEGL:(s + 1) * SEGL], ALU.mult, ALU.add,
                          r=[("pb", pb), ("MOD", l), ("X", m)], w=[("X", m)])

    ALG = S.sb("ALG", [128, 2, 16], F32)
    DTB = S.sb("DTB", [128, 2, 16], F32)
    SSQ = S.sb("SSQ", [128, 2, NT], F32)
    S.dma("sp", ALG[:], alog_rep, writes=["ALG"])
    S.dma("sp", DTB[:], dtb_rep, writes=["DTB"])
    E("act", "activation", ALG[:], ALG[:], AF.Exp, r=["ALG"], w=["ALG"])
    E("dve", "tensor_scalar", ALG[:], ALG[:], -1.0, None, ALU.mult, r=["ALG"], w=["ALG"])
    qs_i = [0]

    def qslot():
        b = 3 + qs_i[0] % 4
        qs_i[0] += 1
        return PB[b][:, 0:128], ("pb", b)

    def gdn_layer(l):
        S.barrier()
        GP = 9620
        NBt, GCt, EGt, EDt, EGLA, EGLB = [scr(GP + 160 * i, 160) for i in range(6)]
        GT0, GT1, GT2 = [scr(10580 + 160 * i, 160) for i in range(3)]
        v3 = lambda ap: ap.rearrange("p (a b) -> p a b", a=16)
        wi = wb_i[0] % NWB
        wb_i[0] += 1
        S.dma("pool", WB[wi][:, :, 0:32], w_in[l, :, 4096:4128].rearrange("(k p) j -> p k j", p=128), writes=[("WB", wi)])
        for t in range(NT):
            for kc in range(KC):
                MM(PB[7][:, t * 32:(t + 1) * 32], H[:, kc, t * 128:(t + 1) * 128], WB[wi][:, kc, 0:32], kc == 0, kc == KC - 1,
                   r=[("WB", wi)] + HK, w=[("pb", 7)])
        pv = PB[7][:, 0:320].rearrange("p (t j) -> p t j", t=NT)
        bview = pv[:, :, 0:16].transpose([0, 2, 1])
        aview = pv[:, :, 16:32].transpose([0, 2, 1])
        E("act", "activation", v3(GT0), bview, AF.Sigmoid, r=[("pb", 7)], w=["GT0"])
        E("dve", "tensor_scalar", NBt, GT0, -1.0, None, ALU.mult, r=["GT0"], w=["NB"])
        E("dve", "tensor_tensor", v3(GT1), aview, DTB[:, l, :].unsqueeze(2).to_broadcast([128, 16, NT]), ALU.add,
          r=[("pb", 7), "DTB"], w=["GT1"])
        E("act", "activation", GT1, GT1, AF.Exp, r=["GT1"], w=["GT1"])
        E("act", "activation", GT1, GT1, AF.Ln, bias=EPS[:, 3:4], r=["GT1", "EPS"], w=["GT1"])
        E("dve", "tensor_tensor", v3(GT2), v3(GT1), ALG[:, l, :].unsqueeze(2).to_broadcast([128, 16, NT]), ALU.mult,
          r=["GT1", "ALG"], w=["GT2"])
        MM(PB[6][:, 0:80], CONST[:, C_MIF, :], GT2[:, 0:80], True, True, r=["CONST", "GT2"], w=[("pb", 6)])
        MM(PB[6][:, 80:160], CONST[:, C_MIB, :], GT2[:, 80:160], True, True, r=["CONST", "GT2"], w=[("pb", 6)])
        MM(PB[6][:, 160:320], CONST[:, C_BLK, :], GT2, True, True, r=["CONST", "GT2"], w=[("pb", 6)])
        MM(PB[6][:, 320:480], CONST[:, C_H0, :], GT2, True, True, r=["CONST", "GT2"], w=[("pb", 6)])
        MM(PB[5][:, 0:160], CONST[:, C_H1, :], GT2, True, True, r=["CONST", "GT2"], w=[("pb", 5)])
        E("dve", "tensor_copy", GCt, PB[6][:, 0:160], r=[("pb", 6)], w=["GC"])
        E("act", "activation", EGt, PB[6][:, 0:160], AF.Exp, r=[("pb", 6)], w=["EG"])
        E("dve", "tensor_tensor", GT0, PB[6][:, 160:320], GCt, ALU.subtract, r=[("pb", 6), "GC", "GT0"], w=["GT0"])
        E("act", "activation", EDt, GT0, AF.Exp, r=["GT0"], w=["ED"])
        E("act", "activation", EGLA, PB[6][:, 320:480], AF.Exp, r=[("pb", 6)], w=["EGL"])
        E("act", "activation", EGLB, PB[5][:, 0:160], AF.Exp, r=[("pb", 5)], w=["EGL"])
        S.barrier()
        for h in range(8):
            gdn_head(l, h, NBt, GCt, EGt, EDt, EGLA, EGLB)
            if h % 4 == 3:
                out_proj(l, h // 4)
        S.barrier()

    def gdn_head(l, h, NBt, GCt, EGt, EDt, EGLA, EGLB):
        CP = scr(0, 1300)
        CP3 = CP.rearrange("p (s w) -> p s w", s=NSEG)
        ACC = scr(1300, T)
        ACC3 = ACC.rearrange("p (s w) -> p s w", s=NSEG)
        SIL = scr(2580, T)
        SQB = scr(3860, T // 2, BF16)
        RN = scr(4500, T)
        QN = scr(5780, T // 2, BF16)
        KN = scr(6420, T // 2, BF16)
        VTM = scr(7060, T, shape=(NT, 128))
        KDF = scr(8340, T // 2, BF16, shape=(NT, 128))
        KDB = scr(8980, T // 2, BF16, shape=(NT, 128))
        R5 = scr(0, T, BF16, shape=(20, 128))
        ATT = scr(1280, T, BF16, shape=(20, 128))
        OTM = scr(2560, T, shape=(NT, 128))
        ZS = scr(3840, T, shape=(NT, 128))
        DIAG, EDM, TMPM, P0, P0T, P1, P1T, RR = [scr(11264 + 128 * i, 128) for i in range(8)]
        NR = scr(5120, 64, BF16)
        VN = scr(5184, 64, BF16)
        TMPS = scr(5248, 128)
        SF = [scr(5376, 128), scr(5504, 128)]
        SBF = [scr(5632, 64, BF16), scr(5696, 64, BF16)]
        S.barrier()
        streams = ((h * 128, h), (1024 + h * 128, 8 + h), (2048 + h * 128, 16 + h))

        def s1_proj(si, bs):
            col, ci = streams[si]
            wt, wk = stream_w(w_in[l, :, col:col + 128])
            return proj_rows(wt, wk, bs=bs)

        def s1_elem(si, bs, cs):
            col, ci = streams[si]
            E("dve", "memset", CP, 0.0, w=["CP"])
            for n, (o, ln) in enumerate(NCH):
                for s in range(o // SEGL, (o + ln) // SEGL):
                    so = s * SEGL - o
                    E("act", "copy", CP3[:, s, 2:258], PB[bs[n]][:, so:so + SEGL], r=[("pb", bs[n])], w=["CP"])
            E("dve", "tensor_scalar", CP3[:, 2:5, 0:2], CP3[:, 1:4, 256:258], FLG[:, 0:1], None, ALU.mult, r=["CP", "FLG"], w=["CP"])
            E("dve", "tensor_scalar", CP3[:, 1:4, 258:260], CP3[:, 2:5, 2:4], FLG[:, 0:1], None, ALU.mult, r=["CP", "FLG"], w=["CP"])
            E("dve", "tensor_scalar", ACC3, CP3[:, :, 0:256], CNV[:, l, ci, 0:1], None, ALU.mult, r=["CP", "CNV"], w=["ACC"])
            for j in range(1, 5):
                E("dve", "scalar_tensor_tensor", ACC3, CP3[:, :, j:j + 256], CNV[:, l, ci, j:j + 1], ACC3, ALU.mult, ALU.add,
                  r=["CP", "CNV", "ACC"], w=["ACC"])
            E("act", "activation", SIL, ACC, AF.Silu, r=["ACC"], w=["SIL"])
            if si < 2:
                E("dve", "tensor_tensor", SQB, SIL, SIL, ALU.mult, r=["SIL"], w=["SQB"])
                b2 = colsum_bcast(SQB, "SQB", bs=cs)
                for n, (o, ln) in enumerate(NCH):
                    rsqrt(RN[:, o:o + ln], PB[b2[n]][:, 0:ln], RMS_EPS, 1.0, ("pb", b2[n]), "RN")
            if si == 0:
                E("dve", "scalar_tensor_tensor", QN, SIL, 128.0 ** -0.5, RN, ALU.mult, ALU.mult, r=["SIL", "RN"], w=["QN"])
            elif si == 1:
                E("dve", "tensor_tensor", SIL, SIL, RN, ALU.mult, r=["SIL", "RN"], w=["SIL"])
                E("act", "copy", KN, SIL, r=["SIL"], w=["KN"])
                for t in range(NT):
                    q, qk = qslot()
                    TR(q, SIL[:, t * 128:(t + 1) * 128], ident, r=["SIL", "CONST"], w=[qk])
                    cf, cb = h * NT + t, (8 + h) * NT + t
                    E("dve", "tensor_scalar", KDF[:, t, :], q, EDt[:, cf:cf + 1], None, ALU.mult, r=[qk, "ED"], w=["KDF"])
                    E("dve", "tensor_scalar", KDB[:, t, :], q, EDt[:, cb:cb + 1], None, ALU.mult, r=[qk, "ED"], w=["KDB"])
            else:
                for t in range(NT):
                    q, qk = qslot()
                    TR(q, SIL[:, t * 128:(t + 1) * 128], ident, r=["SIL", "CONST"], w=[qk])
                    E("act", "copy", VTM[:, t, :], q, r=[qk], w=["VTM"])

        sA, sB = bank_sets
        s1_proj(0, sA)
        s1_proj(1, sB)
        s1_elem(0, sA, sA)
        s1_proj(2, sA)
        s1_elem(1, sB, sB)
        s1_elem(2, sA, None)
        S.barrier()
        rot = [0]

        def rslot():
            bnk = 2 + rot[0] % 5
            rot[0] += 1
            return PB[bnk][:, 0:128], ("pb", bnk)

        ibase = [2560, 3200, 3840, 4480, 5120, 11264]

        def inst_gen(t, d, ib, gq, kq, gk):
            Pa, Pb_, Pc, Pd, RR = [scr(ibase[ib] + 128 * i, 128) for i in range(5)]
            ka, kb_, kc_, kd, kr = [("I", ib, i) for i in range(5)]
            col = (d * 8 + h) * NT + t
            mS = CONST[:, C_MSF if d == 0 else C_MSB, :]
            mI = CONST[:, C_MIF if d == 0 else C_MIB, :]
            E("dve", "tensor_scalar", Pc, ident, GCt[:, col:col + 1], None, ALU.mult, r=["CONST", "GC"], w=[kc_])
            rq, rk = rslot()
            MM(rq, ones_f, Pc, True, True, r=["CONST", kc_], w=[rk])
            E("dve", "tensor_scalar", Pd, rq, GCt[:, col:col + 1], 0.0, ALU.subtract, ALU.min, r=[rk, "GC"], w=[kd])
            yield
            E("act", "activation", Pd, Pd, AF.Exp, r=[kd], w=[kd])
            E("dve", "tensor_tensor", Pd, Pd, mI, ALU.mult, r=[kd, "CONST"], w=[kd])
            yield
            E("dve", "scalar_tensor_tensor", Pa, gq, NBt[:, col:col + 1], Pd, ALU.mult, ALU.mult, r=[gk, "NB", kd], w=[ka])
            E("dve", "tensor_tensor", ATT[:, d * NT + t, :], kq, Pd, ALU.mult, r=[gk, kd], w=[("ATT", d, t)])
            E("dve", "tensor_tensor", Pa, Pa, mS, ALU.mult, r=[ka, "CONST"], w=[ka])
            yield
            tq, tk = rslot()
            TR(tq, Pa, ident, r=[ka, "CONST"], w=[tk])
            E("act", "copy", Pb_, tq, r=[tk], w=[kb_])
            E("dve", "tensor_tensor", RR, Pa, ident, ALU.add, r=[ka, "CONST"], w=[kr])
            yield
            st = {"cur": (Pa, ka), "curT": (Pb_, kb_), "nxt": (Pc, kc_), "nxtT": (Pd, kd)}

            def square(n):
                cur, curT, nxt, nxtT = st["cur"], st["curT"], st["nxt"], st["nxtT"]
                if n < 5:
                    aq, ak = rslot()
                    MM(aq, curT[0], cur[0], True, True, r=[curT[1], cur[1]], w=[ak])
                    E("act", "copy", nxt[0], aq, r=[ak], w=[nxt[1]])
                bq, bk = rslot()
                MM(bq, cur[0], curT[0], True, True, r=[curT[1], cur[1]], w=[bk])
                E("dve", "tensor_copy", nxtT[0], bq, r=[bk], w=[nxtT[1]])
                st["cur"], st["curT"], st["nxt"], st["nxtT"] = nxt, nxtT, cur, curT

            def update(n):
                PT = st["curT"]
                cq, ck = rslot()
                MM(cq, PT[0], RR, True, True, r=[PT[1], kr], w=[ck])
                if n < 5:
                    E("dve", "tensor_tensor", RR, RR, cq, ALU.add, r=[kr, ck], w=[kr])
                else:
                    E("dve", "tensor_tensor", R5[:, d * NT + t, :], RR, cq, ALU.add, r=[kr, ck], w=[("R5", d, t)])

            square(1)
            yield
            for n in range(1, 6):
                update(n)
                if n < 5:
                    square(n + 1)
                yield

        for tiles in ((0, 1, 2), (3, 4, 5), (6, 7), (8, 9)):
            gens = []
            for j, t in enumerate(tiles):
                tl = slice(t * 128, (t + 1) * 128)
                gb = 0 if j < 2 else 1
                gq = PB[gb][:, (2 * (j % 2)) * 128:(2 * (j % 2) + 1) * 128]
                kq = PB[gb][:, (2 * (j % 2) + 1) * 128:(2 * (j % 2) + 2) * 128]
                MM(gq, KN[:, tl], KN[:, tl], True, True, r=["KN"], w=[("pb", gb)])
                MM(kq, KN[:, tl], QN[:, tl], True, True, r=["KN", "QN"], w=[("pb", gb)])
                for d in range(2):
                    gens.append(inst_gen(t, d, 2 * j + d, gq, kq, ("pb", gb)))
            while gens:
                for g_ in list(gens):
                    try:
                        next(g_)
                    except StopIteration:
                        gens.remove(g_)
        S.barrier()
        NRs = [NR, scr(11264, 64, BF16)]
        VNs = [VN, scr(11328, 64, BF16)]
        TMPs = [TMPS, scr(11392, 128)]
        for d in range(2):
            E("dve", "memset", NRs[d], 0.0, w=[("NR", d)])
            E("dve", "memset", VNs[d], 0.0, w=[("VN", d)])
        E("dve", "memset", OTM.rearrange("p a b -> p (a b)"), 0.0, w=[("OTM", t) for t in range(NT)])
        crot = [0]

        def cslot():
            bnk = crot[0] % 7
            crot[0] += 1
            return PB[bnk][:, 0:128], ("pb", bnk)

        order = [[(t, e) for t in range(NT) for e in (0, 1)], [(t, e) for t in range(NT - 1, -1, -1) for e in (1, 0)]]
        EGLx = (EGLA, EGLB)
        def chain_step(k, d):
            t, e = order[d][k]
            s = t // 2
            tl = slice(t * 128, (t + 1) * 128)
            rows = slice(e * 64, (e + 1) * 64)
            col = (d * 8 + h) * NT + t
            SK, SBK, NK, VK, TK = ("SF", d), ("SBF", d), ("NR", d), ("VN", d), ("TMPS", d)
            NR_, VN_, TM_ = NRs[d], VNs[d], TMPs[d]
            seg_start = (t % 2 == 0 and e == 0) if d == 0 else (t % 2 == 1 and e == 1)
            seg_end = (t % 2 == 1 and e == 1) if d == 0 else (t % 2 == 0 and e == 0)
            if seg_start:
                if s == 0:
                    E("dve", "memset", SF[d], 0.0, w=[SK])
                elif (d == 0 and s == 1) or (d == 1 and s == 4):
                    S.dma("sp", SF[d], sinit[l, d, h], writes=[SK])
                else:
                    E("dve", "tensor_scalar", SF[d], SF[d], FLG[:, 0:1], None, ALU.mult, r=[SK, "FLG"], w=[SK])
                E("act", "copy", SBF[d], SF[d], r=[SK], w=[SBK])
            q1, k1 = cslot()
            MM(q1, KN[:, tl], SBF[d], True, True, r=["KN", SBK], w=[k1])
            E("dve", "scalar_tensor_tensor", NR_[rows, :], q1[rows, :], EGt[rows, col:col + 1], VTM[rows, t, :], ALU.mult, ALU.subtract,
              r=[k1, "EG", "VTM"], w=[NK])
            q3, k3 = cslot()
            MM(q3, QN[:, tl], SBF[d], True, True, r=["QN", SBK], w=[k3])
            E("act", "activation", TM_[rows, :], q3[rows, :], AF.Identity, scale=EGt[rows, col:col + 1], r=[k3, "EG"], w=[TK])
            yield
            q2, k2 = cslot()
            MM(q2, R5[:, d * NT + t, :], NR_, True, True, r=[("R5", d, t), NK], w=[k2])
            E("dve", "tensor_scalar", VN_[rows, :], q2[rows, :], NBt[rows, col:col + 1], None, ALU.mult, r=[k2, "NB"], w=[VK])
            yield
            q5, k5 = cslot()
            KD = KDF if d == 0 else KDB
            MM(q5, KD[rows, t, :], VN_[rows, :], True, True, r=["KDF", "KDB", VK], w=[k5])
            E("dve", "scalar_tensor_tensor", SBF[d], SF[d], EGLx[e][:, col:col + 1], q5, ALU.mult, ALU.add, r=[SK, "EGL", k5], w=[SBK])
            E("dve", "scalar_tensor_tensor", SF[d], SF[d], EGLx[e][:, col:col + 1], q5, ALU.mult, ALU.add, r=[SK, "EGL", k5], w=[SK])
            yield
            q4, k4 = cslot()
            MM(q4, ATT[:, d * NT + t, :], VN_, True, True, r=[("ATT", d, t), VK], w=[k4])
            E("dve", "tensor_tensor", TM_[rows, :], TM_[rows, :], q4[rows, :], ALU.add, r=[TK, k4], w=[TK])
            E("dve", "tensor_tensor", OTM[rows, t, :], OTM[rows, t, :], TM_[rows, :], ALU.add, r=[TK, ("OTM", t)], w=[("OTM", t)])
            if seg_end:
                stg = scr(11520 + 128 * (2 * d + s % 2), 128)
                E("act", "copy", stg, SF[d], r=[SK], w=[("STG", d, s % 2)])
                S.dma("sp", nst[l, s, d, h], stg, reads=[("STG", d, s % 2)], writes=[("nst", l, s, d, h)])
                outkeys.append(("nst", l, s, d, h))

        wz, wzk = stream_w(w_in[l, :, 3072 + h * 128:3072 + (h + 1) * 128])
        wb_pinned.add(wzk[1])
        for k in range(2 * NT):
            if l + 1 < nlayers:
                adaln_tiles(l + 1, 1, 7)
            if k % 2 == 0:
                tz = k // 2
                zq, zk = cslot()
                for kc in range(KC):
                    MM(zq, H[:, kc, tz * 128:(tz + 1) * 128], wz[:, kc, :], kc == 0, kc == KC - 1, r=[wzk] + HK, w=[zk])
                E("act", "activation", ZS[:, tz, :], zq, AF.Silu, r=[zk], w=["ZS"])
            gens = [chain_step(k, 0), chain_step(k, 1)]
            while gens:
                for g_ in list(gens):
                    try:
                        next(g_)
                    except StopIteration:
                        gens.remove(g_)
        wb_pinned.discard(wzk[1])
        S.barrier()
        for t in range(NT):
            E("act", "activation", TMPS, OTM[:, t, :], AF.Square, accum_out=SSQ[:, 0, t:t + 1], r=[("OTM", t)], w=["TMPS", "SSQ"])
        rsqrt(SSQ[:, 1, :], SSQ[:, 0, :], RMS_EPS, 1.0 / 128, "SSQ", "SSQ")
        OK_ = [("OTM", t) for t in range(NT)]
        E("dve", "tensor_tensor", OTM, OTM, SSQ[:, 1, :].unsqueeze(2).to_broadcast([128, NT, 128]), ALU.mult, r=OK_ + ["SSQ"], w=OK_)
        E("dve", "tensor_tensor", OTM, OTM, GNW[:, l, :].unsqueeze(1).to_broadcast([128, NT, 128]), ALU.mult, r=OK_ + ["GNW"], w=OK_)
        E("dve", "tensor_tensor", OTM, OTM, ZS, ALU.mult, r=OK_ + ["ZS"], w=OK_)
        for t in range(NT):
            oq, ok = qslot()
            TR(oq, OTM[:, t, :], ident, r=[("OTM", t), "CONST"], w=[ok])
            E("act", "copy", OTG[:, h % 4, t * 128:(t + 1) * 128], oq, r=[ok], w=["OTG"])

    def qk_prep(l, col, which, raw, sqb, rn, nrm, outb, ropec, ropes):
        wt, wk = stream_w(w_in[l, :, col:col + 128])
        bs = proj_rows(wt, wk)
        for n, (o, ln) in enumerate(NCH):
            E("act", "copy", raw[:, o:o + ln], PB[bs[n]][:, 0:ln], r=[("pb", bs[n])], w=["A_raw"])
        E("dve", "tensor_tensor", sqb, raw, raw, ALU.mult, r=["A_raw"], w=["A_sq"])
        b2 = colsum_bcast(sqb, "A_sq")
        for n, (o, ln) in enumerate(NCH):
            rsqrt(rn[:, o:o + ln], PB[b2[n]][:, 0:ln], RMS_EPS, 1.0 / 128, ("pb", b2[n]), "A_rn")
        E("dve", "scalar_tensor_tensor", nrm, raw, QKW[:, l, which:which + 1], rn, ALU.mult, ALU.mult,
          r=["A_raw", "QKW", "A_rn"], w=["A_nrm"])
        b3 = bank_sets[bs_i[0] % 2]
        bs_i[0] += 1
        for n, (o, ln) in enumerate(NCH):
            MM(PB[b3[n]][:, 0:ln], CONST[:, C_PERM, :], nrm[:, o:o + ln], True, True, r=["CONST", "A_nrm"], w=[("pb", b3[n])])
        E("dve", "tensor_tensor", raw, nrm, ropec, ALU.mult, r=["A_nrm", "ROPE"], w=["A_raw"])
        for n, (o, ln) in enumerate(NCH):
            E("dve", "tensor_tensor", rn[:, o:o + ln], PB[b3[n]][:, 0:ln], ropes[:, o:o + ln], ALU.mult,
              r=[("pb", b3[n]), "ROPE"], w=["A_rn"])
        E("dve", "tensor_tensor", outb, raw, rn, ALU.add, r=["A_raw", "A_rn"], w=[("A_out", id(outb))])

    def attn_layer(l):
        S.barrier()
        ROPEC = scr(0, T)
        ROPES = scr(1280, T)
        RAW = scr(2560, T)
        SQB = scr(3840, T // 2, BF16)
        RN = scr(4480, T)
        NRM = scr(5760, T)
        KRB = scr(7040, T // 2, BF16)
        VTM = scr(7680, T // 2, BF16, shape=(NT, 128))
        CKB = scr(8320, 256, BF16)
        CVB = scr(8576, 256, BF16, shape=(4, 128))
        QRB = scr(8832, T // 2, BF16)
        VA = scr(9472, T)
        PT = [scr(10752 + 256 * i, 256, BF16) for i in range(4)]
        S.dma("sp", ROPEC, ropeC_in, writes=["ROPE"])
        S.dma("sp", ROPES, ropeS_in, writes=["ROPE"])
        sc = 128.0 ** -0.5
        kq, oq = id(KRB), id(QRB)
        for g in range(2):
            qk_prep(l, 5152 + g * 128, 1, RAW, SQB, RN, NRM, KRB, ROPEC, ROPES)
            S.dma("sp", nk[l, g], NRM, reads=["A_nrm"], writes=[("nk", l, g)])
            outkeys.append(("nk", l, g))
            wt, wk = stream_w(w_in[l, :, 5408 + g * 128:5408 + (g + 1) * 128])
            bs = proj_rows(wt, wk)
            for n, (o, ln) in enumerate(NCH):
                E("act", "copy", VA[:, o:o + ln], PB[bs[n]][:, 0:ln], r=[("pb", bs[n])], w=["A_va"])
            S.dma("sp", nv[l, g], VA, reads=["A_va"], writes=[("nv", l, g)])
            outkeys.append(("nv", l, g))
            for t in range(NT):
                pb, qd = 6 + (t // 4) % 2, t % 4
                TR(PB[pb][:, qd * 128:(qd + 1) * 128], VA[:, t * 128:(t + 1) * 128], ident, r=["A_va", "CONST"], w=[("pb", pb)])
                E("act", "copy", VTM[:, t, :], PB[pb][:, qd * 128:(qd + 1) * 128], r=[("pb", pb)], w=["A_vtm"])
            S.dma("pool", CKB, ckT[l, g], writes=["A_ck"])
            S.dma("pool", CVB, cvv[l, g].rearrange("(b p) d -> p b d", p=128), writes=["A_cv"])
            for hl in range(4):
                hq = 4 * g + hl
                qk_prep(l, 4128 + hq * 128, 0, RAW, SQB, RN, NRM, QRB, ROPEC, ROPES)
                jobs = [(0, 256, [("l", kb) for kb in (0, 1)])]
                for qc in range(2):
                    jobs.append((256 + qc * 512, 512, [("l", kb) for kb in range(2, 10)] + [("c", cb) for cb in range(4)]))
                for (q0, qn, blocks) in jobs:
                    nb = len(blocks)
                    STB = (0, 1, 4, 5)

                    def emit_st(bi):
                        kind, kb = blocks[bi]
                        sl_ = bi % 4
                        pb = STB[sl_]
                        if kind == "l":
                            klhs, kr = KRB[:, kb * 128:(kb + 1) * 128], ("A_out", kq)
                        else:
                            klhs, kr = CKB[:, kb * 128:(kb + 1) * 128], "A_ck"
                        MM(PB[pb][:, 0:qn], klhs, QRB[:, q0:q0 + qn], True, True, r=[kr, ("A_out", oq)], w=[("pb", pb)])
                        for hh in range(qn // 256):
                            qs = (q0 + hh * 256) // 256
                            same = (kind == "l") and (kb // 2 == qs)
                            E("act", "activation", PT[sl_][:, hh * 256:(hh + 1) * 256], PB[pb][:, hh * 256:(hh + 1) * 256], AF.Exp,
                              bias=(EPS[:, 2:3] if same else FLG[:, 1:2]), scale=sc, r=[("pb", pb), "FLG", "EPS"], w=[("PT", sl_)])

                    def emit_pv(bi):
                        kind, kb = blocks[bi]
                        sl_ = bi % 4
                        if kind == "l":
                            vlhs, vr = VTM[:, kb, :], "A_vtm"
                        else:
                            vlhs, vr = CVB[:, kb, :], "A_cv"
                        MM(PB[2][:, 0:qn], vlhs, PT[sl_][:, 0:qn], bi == 0, bi == nb - 1, r=[vr, ("PT", sl_)], w=[("pb", 2)])
                        MM(PB[3][:, 0:qn], ones_b, PT[sl_][:, 0:qn], bi == 0, bi == nb - 1, r=["CONSTB", ("PT", sl_)], w=[("pb", 3)])

                    LA = 2
                    for bi in range(min(LA, nb)):
                        emit_st(bi)
                    for bi in range(nb):
                        if bi + LA < nb:
                            emit_st(bi + LA)
                        emit_pv(bi)
                    E("dve", "reciprocal", RAW[:, 0:qn], PB[3][:, 0:qn], r=[("pb", 3)], w=["A_raw"])
                    E("dve", "tensor_tensor", OTG[:, hl, q0:q0 + qn], PB[2][:, 0:qn], RAW[:, 0:qn], ALU.mult,
                      r=[("pb", 2), "A_raw"], w=["OTG"])
            out_proj(l, 2 + g)
        S.barrier()

    for l in range(nlayers):
        modulate(l, 0, 1)
        if do_gdn:
            gdn_layer(l)
        if do_attn:
            attn_layer(l)
        if not (do_gdn or do_attn):
            pass
        S.barrier()
        layer_norm(l, 1, False)
        modulate(l, 3, 4)
        S.barrier()
        ffn(l)
        S.barrier()
        layer_norm(l, 2, l == nlayers - 1)
        S.barrier()

    for c in range(KC):
        S.dma("sp", yT[c * 128:(c + 1) * 128, :], X[:, c, :], reads=[("X", c)], writes=[("yT", c)])
        outkeys.append(("yT", c))
    S.finish(final_keys=outkeys)
    return nc


def _core_segments(core):
    if core < 6:
        return [("p", 5 * core + s, 0) for s in range(5)]
    b = core - 6
    return [("p", 30 + b, 0)] + [("s", b, q) for q in range(4)]


def _rope_tables(core):
    C = np.ones((128, T), np.float32)
    Sg = np.zeros((128, T), np.float32)
    if core >= 6:
        pos = np.arange(1024)
        row = (pos // 64).astype(np.float32)
        col = (pos % 64).astype(np.float32)
        inv = (np.float32(10000.0) ** (-np.arange(0, 64, 2, dtype=np.float32) / np.float32(64))).astype(np.float32)
        d = np.arange(128)
        ang = np.where((d < 64)[:, None], row[None, :] * inv[d % 32][:, None], col[None, :] * inv[d % 32][:, None])
        ang = ang.astype(np.float32)
        sign = np.where((d % 64) < 32, -1.0, 1.0).astype(np.float32)[:, None]
        C[:, 256:] = np.cos(ang)
        Sg[:, 256:] = np.sin(ang) * sign
    return C, Sg


def make_in_maps(inp):
    f = np.float32
    g = lambda k: np.asarray(inp[k], dtype=f)
    x_prompt, x_sample = g("x_prompt"), g("x_sample")
    cache_k, cache_v, state_gdn = g("cache_k"), g("cache_v"), g("state_gdn")
    c, c_ctx = g("c"), g("c_ctx")
    shared = {
        "w_ada": g("w_ada"), "w_in": g("w_in"), "w_o": g("w_o"), "w_gu": g("w_gate_up"), "w_dn": g("w_down"),
        "b_adaT": np.ascontiguousarray(g("b_ada").reshape(2, 96, 128).transpose(2, 0, 1)),
        "convT": np.ascontiguousarray(g("conv_w").reshape(2, 5, 24, 128).transpose(3, 0, 2, 1)),
        "alog_rep": np.ascontiguousarray(np.broadcast_to(g("a_log").reshape(1, 2, 16), (128, 2, 16))),
        "dtb_rep": np.ascontiguousarray(np.broadcast_to(g("dt_bias").reshape(1, 2, 16), (128, 2, 16))),
        "gnw_rep": np.ascontiguousarray(np.broadcast_to(g("gdn_norm_w").reshape(1, 2, 128), (128, 2, 128))),
        "qknw": np.ascontiguousarray(np.stack([g("q_norm_w"), g("k_norm_w")], 0).transpose(2, 1, 0)),
        "lnp": np.ascontiguousarray(np.stack([g("ln1_g"), g("ln1_b"), g("ln2_g"), g("ln2_b")], 0)
                                    .reshape(4, 2, 16, 128).transpose(3, 1, 0, 2)),
        "consts": make_consts(),
    }
    maps = []
    for core in range(8):
        segs = _core_segments(core)
        xs, cs = [], []
        for kind, i, q in segs:
            if kind == "p":
                xs.append(x_prompt[i])
                cs.append(c_ctx)
            else:
                xs.append(x_sample[i, q * 256:(q + 1) * 256])
                cs.append(c[i])
        m = dict(shared)
        m["xT"] = np.ascontiguousarray(np.concatenate(xs, 0).T)
        m["condT"] = np.ascontiguousarray(np.stack(cs, 0).reshape(5, 16, 128).transpose(2, 1, 0))
        if core >= 6:
            b = core - 6
            m["ckT"] = np.ascontiguousarray(cache_k[b].transpose(0, 2, 3, 1))
            m["cvv"] = np.ascontiguousarray(cache_v[b].transpose(0, 2, 1, 3))
            m["sinit"] = np.ascontiguousarray(state_gdn[b])
            m["flags"] = np.ascontiguousarray(np.broadcast_to(np.array([1.0, 0.0], f), (128, 2)))
        else:
            m["ckT"] = np.zeros((2, 2, 128, 512), f)
            m["cvv"] = np.zeros((2, 2, 512, 128), f)
            m["sinit"] = np.zeros((2, 2, 8, 128, 128), f)
            m["flags"] = np.ascontiguousarray(np.broadcast_to(np.array([0.0, NEG], f), (128, 2)))
        m["ropeC"], m["ropeS"] = _rope_tables(core)
        maps.append(m)
    return maps


def assemble(results):
    f = np.float32
    y_p = np.zeros((32, 256, D), f)
    y_s = np.zeros((2, 1024, D), f)
    ck = np.zeros((32, 2, 256, 2, 128), f)
    cv = np.zeros((32, 2, 256, 2, 128), f)
    st = np.zeros((32, 2, 2, 8, 128, 128), f)
    for core in range(8):
        r = results[core]
        y = np.asarray(r["yT"]).T
        nk_ = np.asarray(r["nk"])
        nv_ = np.asarray(r["nv"])
        ns_ = np.asarray(r["nst"])
        for s, (kind, i, q) in enumerate(_core_segments(core)):
            sl = slice(s * 256, (s + 1) * 256)
            if kind == "p":
                y_p[i] = y[sl]
                ck[i] = nk_[:, :, :, sl].transpose(0, 3, 1, 2)
                cv[i] = nv_[:, :, :, sl].transpose(0, 3, 1, 2)
                st[i] = ns_[:, s]
            else:
                y_s[i, q * 256:(q + 1) * 256] = y[sl]
    return y_p, y_s, ck, cv, st


_NC_CACHE = {}


def kernel(**inputs):
    maps = make_in_maps(inputs)
    if "nc" not in _NC_CACHE:
        _NC_CACHE["nc"] = build_program(json.loads(os.environ.get("KCFG", "{}")))
    res = run_bass_kernel_spmd(_NC_CACHE["nc"], maps, core_ids=list(range(8)))
    return assemble(res.results)
```

```python
import os
import json
import numpy as np
import concourse.bass as bass
import concourse.mybir as mybir
from concourse.bass_utils import run_bass_kernel_spmd

F32 = mybir.dt.float32
BF16 = mybir.dt.bfloat16
AF = mybir.ActivationFunctionType
ALU = mybir.AluOpType


class _Op:
    __slots__ = ("fn", "deps", "dma", "needed", "cnt", "dsem", "dval")

    def __init__(self, fn, deps, dma):
        self.fn = fn
        self.deps = deps
        self.dma = dma
        self.needed = False
        self.cnt = 0
        self.dsem = None
        self.dval = 0


class Sched:
    ENGINES = ("pe", "act", "dve", "pool", "sp")
    NDMA = 8
    SAME_ENGINE_SYNC = ("act", "dve", "pool")

    def __init__(self, nc):
        self.nc = nc
        self.ops = {e: [] for e in self.ENGINES}
        self.res_w = {}
        self.res_r = {}
        self.nalloc = 0

    def sb(self, name, shape, dtype):
        return self.nc.alloc_sbuf_tensor(name, list(shape), dtype)

    def ps(self, name, shape, dtype=F32):
        return self.nc.alloc_psum_tensor(name, list(shape), dtype)

    def op(self, eng, fn, reads=(), writes=(), dma=False):
        ops = self.ops[eng]
        me = (eng, len(ops))
        writes = list(writes) + [k for k in reads if isinstance(k, tuple) and k and k[0] == "pb"]
        deps = set()
        for k in reads:
            w = self.res_w.get(k)
            if w is not None:
                deps.add(w)
        for k in writes:
            w = self.res_w.get(k)
            if w is not None:
                deps.add(w)
            for r in self.res_r.get(k, ()):
                deps.add(r)
        deps.discard(me)
        for k in writes:
            self.res_w[k] = me
            self.res_r[k] = []
        for k in reads:
            self.res_r.setdefault(k, []).append(me)
        ops.append(_Op(fn, deps, dma))
        return me

    def barrier(self):
        deps = set()
        for e in self.ENGINES:
            ops = self.ops[e]
            if not ops:
                continue
            for i in range(len(ops) - 1, -1, -1):
                if ops[i].fn is not None:
                    deps.add((e, i))
                    break
            nd = 0
            for i in range(len(ops) - 1, -1, -1):
                if ops[i].dma:
                    deps.add((e, i))
                    nd += 1
                    if nd >= self.NDMA:
                        break
        for e in self.ENGINES:
            self.ops[e].append(_Op(None, set(d for d in deps if not (d[0] == e and not self.ops[d[0]][d[1]].dma)), False))

    def dma(self, eng, out, in_, reads=(), writes=()):
        return self.op(eng, lambda e: e.dma_start(out=out, in_=in_), reads, writes, dma=True)

    def finish(self, final_keys=()):
        nc = self.nc
        fin = set()
        for k in final_keys:
            w = self.res_w.get(k)
            if w is not None:
                fin.add(w)
        self.ops["sp"].append(_Op(None, fin, False))
        for e in self.ENGINES:
            for o in self.ops[e]:
                for (e2, i2) in o.deps:
                    self.ops[e2][i2].needed = True
        EPOCH = 16000
        esem = {e: [nc.alloc_semaphore("s_%s%d" % (e, i)) for i in range(1 + len(self.ops[e]) // EPOCH)]
                for e in self.ENGINES}
        dsems = {e: [nc.alloc_semaphore("d_%s%d" % (e, i)) for i in range(self.NDMA)]
                 for e in self.ENGINES if any(o.dma for o in self.ops[e])}
        for e in self.ENGINES:
            c = 0
            k = 0
            for o in self.ops[e]:
                if o.dma:
                    o.dsem = dsems[e][k % self.NDMA]
                    o.dval = 16 * (k // self.NDMA + 1)
                    k += 1
                elif o.needed and o.fn is not None:
                    c += 1
                    o.dsem = esem[e][(c - 1) // EPOCH]
                    o.cnt = (c - 1) % EPOCH + 1
        handles = {"pe": "tensor", "act": "scalar", "dve": "vector", "pool": "gpsimd", "sp": "sync"}
        streams = {}
        for e in self.ENGINES:
            st = []
            waited = {}

            def wait(sem, val, st=st, waited=waited):
                key = id(sem)
                if waited.get(key, 0) >= val:
                    return
                waited[key] = val
                st.append(("w", sem, val))

            for o in self.ops[e]:
                need = {}
                for (e2, i2) in sorted(o.deps):
                    d = self.ops[e2][i2]
                    if d.dma:
                        need[id(d.dsem)] = (d.dsem, max(d.dval, need.get(id(d.dsem), (None, 0))[1]))
                    else:
                        if e2 == e and e not in self.SAME_ENGINE_SYNC:
                            continue
                        assert d.cnt > 0, (e, e2, i2)
                        need[id(d.dsem)] = (d.dsem, max(d.cnt, need.get(id(d.dsem), (None, 0))[1]))
                for sem, v in need.values():
                    wait(sem, v)
                if o.dma and o.dval > 16:
                    wait(o.dsem, o.dval - 16)
                if o.fn is None:
                    continue
                if o.dma:
                    st.append(("i", o.fn, o.dsem, 16))
                elif o.needed:
                    st.append(("i", o.fn, o.dsem, 1))
                else:
                    st.append(("i", o.fn, None, 0))
            streams[e] = st
        val = {}
        pos = {e: 0 for e in self.ENGINES}
        progress = True
        while progress:
            progress = False
            for e in self.ENGINES:
                st = streams[e]
                while pos[e] < len(st):
                    it = st[pos[e]]
                    if it[0] == "w":
                        if val.get(id(it[1]), 0) < it[2]:
                            break
                    elif it[2] is not None:
                        val[id(it[2])] = val.get(id(it[2]), 0) + it[3]
                    pos[e] += 1
                    progress = True
        stuck = {e: (pos[e], len(streams[e])) for e in self.ENGINES if pos[e] < len(streams[e])}
        assert not stuck, "deadlock in schedule: %r" % (stuck,)
        self.stats = {e: (len(streams[e]), sum(1 for it in streams[e] if it[0] == "w")) for e in self.ENGINES}
        with nc.Block() as block:
            for e in self.ENGINES:
                st = streams[e]
                if not st:
                    continue

                def body(eng, st=st):
                    for it in st:
                        if it[0] == "w":
                            eng.wait_ge(it[1], it[2])
                        else:
                            ins = it[1](eng)
                            if it[2] is not None:
                                ins.then_inc(it[2], it[3])

                getattr(block, handles[e])(body)


D = 2048
T = 1280
NSEG = 5
SEGL = 256
KC = 16
NT = 10
NCH = ((0, 512), (512, 512), (1024, 256))
DFF = 5632
IN_W = 5664
ALPHA = 4.0 ** 0.25
RMS_EPS = 1e-6
LN_EPS = 1e-5
NEG = -30000.0
C_ID, C_ONE, C_MSF, C_MIF, C_MSB, C_MIB, C_BLK, C_H0, C_H1, C_PERM = range(10)
NCONST = 10


def make_consts():
    i = np.arange(128)
    a = i[:, None]
    b = i[None, :]
    same = (a // 64) == (b // 64)
    c = np.zeros((128, NCONST, 128), np.float32)
    c[:, C_ID] = (a == b)
    c[:, C_ONE] = 1.0
    c[:, C_MSF] = same & (a < b)
    c[:, C_MIF] = same & (a <= b)
    c[:, C_MSB] = same & (a > b)
    c[:, C_MIB] = same & (a >= b)
    c[:, C_BLK] = same
    c[:, C_H0] = (a < 64) & (b >= 0)
    c[:, C_H1] = (a >= 64) & (b >= 0)
    c[:, C_PERM] = (a == (b ^ 32))
    return c


def build_program(cfg=None):
    cfg = dict(cfg or {})
    do_gdn = cfg.get("gdn", True)
    do_attn = cfg.get("attn", True)
    nlayers = cfg.get("layers", 2)
    nc = bass.Bass("TRN2", target_bir_lowering=False)

    def din(name, shape):
        return nc.dram_tensor(name, list(shape), F32, kind="ExternalInput").ap()

    def dout(name, shape):
        return nc.dram_tensor(name, list(shape), F32, kind="ExternalOutput").ap()

    xT = din("xT", [D, T])
    condT = din("condT", [128, KC, NSEG])
    w_ada = din("w_ada", [2, D, 6 * D])
    b_adaT = din("b_adaT", [128, 2, 96])
    w_in = din("w_in", [2, D, IN_W])
    convT = din("convT", [128, 2, 24, 5])
    alog_rep = din("alog_rep", [128, 2, 16])
    dtb_rep = din("dtb_rep", [128, 2, 16])
    gnw_rep = din("gnw_rep", [128, 2, 128])
    qknw = din("qknw", [128, 2, 2])
    w_o = din("w_o", [2, D, D])
    lnp_in = din("lnp", [128, 2, 4, KC])
    w_gu = din("w_gu", [2, D, 2 * DFF])
    w_dn = din("w_dn", [2, DFF, D])
    ckT = din("ckT", [2, 2, 128, 512])
    cvv = din("cvv", [2, 2, 512, 128])
    sinit = din("sinit", [2, 2, 8, 128, 128])
    ropeC_in = din("ropeC", [128, T])
    ropeS_in = din("ropeS", [128, T])
    flags_in = din("flags", [128, 2])
    consts_in = din("consts", [128, NCONST, 128])

    yT = dout("yT", [D, T])
    nk = dout("nk", [2, 2, 128, T])
    nv = dout("nv", [2, 2, 128, T])
    nst = dout("nst", [2, NSEG, 2, 8, 128, 128])

    S = Sched(nc)
    outkeys = []

    def E(eng, meth, *args, r=(), w=(), **kw):
        S.op(eng, lambda e: getattr(e, meth)(*args, **kw), r, w)

    def MM(out, lhsT, rhs, start, stop, r=(), w=()):
        S.op("pe", lambda e: e.matmul(out, lhsT, rhs, start=start, stop=stop), r, w)

    def rsqrt(out, in_, eps, scale, rk, wk):
        E("act", "activation", out, in_, AF.Ln, bias=EPS[:, {1e-6: 0, 1e-5: 1}[eps]:{1e-6: 1, 1e-5: 2}[eps]], scale=scale,
          r=[rk, "EPS"], w=[wk])
        E("act", "activation", out, out, AF.Exp, scale=-0.5, r=[wk], w=[wk])

    def TR(out, in_, ident, r=(), w=()):
        S.op("pe", lambda e: e.transpose(out, in_, ident), r, w)

    X = S.sb("X", [128, KC, T], F32)
    H = S.sb("H", [128, KC, T], BF16)
    OTG = S.sb("OTG", [128, 4, T], BF16)
    CONST = S.sb("CONST", [128, NCONST, 128], F32)
    CONSTB = S.sb("CONSTB", [128, 2, 128], BF16)
    MOD = S.sb("MOD", [128, 2, 96, NSEG], F32)
    LNP = S.sb("LNP", [128, 2, 4, KC], F32)
    LNPS = S.sb("LNPS", [128, 2, 4, KC], F32)
    FLG = S.sb("FLG", [128, 2], F32)
    CNV = S.sb("CNV", [128, 2, 24, 5], F32)
    GNW = S.sb("GNW", [128, 2, 128], F32)
    QKW = S.sb("QKW", [128, 2, 2], F32)
    NWB = 3
    WB = [S.sb("WB%d" % i, [128, KC, 128], BF16) for i in range(NWB)]
    PB = [S.ps("pb%d" % i, [128, 512], F32) for i in range(8)]
    SCR = S.sb("SCR", [128, 12288], F32)

    EPS = S.sb("EPS", [128, 4], F32)
    E("dve", "memset", EPS[:, 0:1], RMS_EPS, w=["EPS"])
    E("dve", "memset", EPS[:, 1:2], LN_EPS, w=["EPS"])
    E("dve", "memset", EPS[:, 2:3], 0.0, w=["EPS"])
    E("dve", "memset", EPS[:, 3:4], 1.0, w=["EPS"])
    ident = CONST[:, C_ID, :]
    ones_f = CONST[:, C_ONE, :]
    ident_b = CONSTB[:, 0, :]
    ones_b = CONSTB[:, 1, :]

    S.dma("sp", CONST[:], consts_in, writes=["CONST"])
    S.dma("sp", X[:], xT.rearrange("(c p) t -> p c t", p=128), writes=[("X", c) for c in range(KC)])
    S.dma("sp", LNP[:], lnp_in, writes=["LNP"])
    S.dma("sp", FLG[:], flags_in, writes=["FLG"])
    S.dma("sp", CNV[:], convT, writes=["CNV"])
    S.dma("sp", GNW[:], gnw_rep, writes=["GNW"])
    S.dma("sp", QKW[:], qknw, writes=["QKW"])
    E("dve", "tensor_copy", CONSTB[:, 0, :], ident, r=["CONST"], w=["CONSTB"])
    E("dve", "tensor_copy", CONSTB[:, 1, :], ones_f, r=["CONST"], w=["CONSTB"])
    E("dve", "tensor_scalar", LNPS[:], LNP[:], ALPHA, None, ALU.mult, r=["LNP"], w=["LNPS"])

    wb_i = [0]
    wb_pinned = set()

    def stream_w(src_ap, nk_=KC):
        i = wb_i[0] % NWB
        while i in wb_pinned:
            wb_i[0] += 1
            i = wb_i[0] % NWB
        wb_i[0] += 1
        S.dma("pool", WB[i][:, 0:nk_, :], src_ap.rearrange("(k p) j -> p k j", p=128),
              writes=[("WB", i)])
        return WB[i], ("WB", i)

    def scr(off, n, dtype=F32, shape=None):
        ap = SCR[:, off:off + n]
        if dtype == BF16:
            ap = SCR[:, off:off + n].bitcast(BF16)
        if shape is not None:
            ap = ap.rearrange("p (a b) -> p a b", a=shape[0])
        return ap

    CT = scr(0, KC * NSEG)
    CSt = S.sb("CSt", [128, KC * NSEG], BF16)
    BADt = S.sb("BADt", [128, 2 * 96], F32)
    CS = CSt[:]
    BAD = BADt[:]
    S.dma("sp", CT.rearrange("p (c s) -> p c s", c=KC), condT, writes=["CT"])
    S.dma("sp", BAD.rearrange("p (l j) -> p l j", l=2), b_adaT, writes=["BAD"])
    E("act", "activation", CS, CT, AF.Silu, r=["CT"], w=["CS"])
    CS3 = CS.rearrange("p (c s) -> p c s", c=KC)
    ada_next = {}

    def adaln_tiles(l, n, bank):
        j = ada_next.get(l, 0)
        j1 = min(96, j + n)
        for jj in range(j, j1):
            wt, wk = stream_w(w_ada[l, :, jj * 128:(jj + 1) * 128])
            for kc in range(KC):
                MM(PB[bank][:, jj * NSEG:(jj + 1) * NSEG], wt[:, kc, :], CS3[:, kc, :], kc == 0, kc == KC - 1,
                   r=[wk, "CS"], w=[("pb", bank)])
        ada_next[l] = j1
        if j1 == 96 and j < 96:
            E("dve", "tensor_tensor", MOD[:, l, :, :], PB[bank][:, 0:96 * NSEG].rearrange("p (j s) -> p j s", j=96),
              BAD[:, l * 96:(l + 1) * 96].unsqueeze(2).to_broadcast([128, 96, NSEG]), ALU.add,
              r=[("pb", bank), "BAD"], w=[("MOD", l)])
            for g in (1, 4):
                E("dve", "tensor_scalar", MOD[:, l, g * 16:(g + 1) * 16, :], MOD[:, l, g * 16:(g + 1) * 16, :],
                  1.0, 1.0 / ALPHA, ALU.add, ALU.mult, r=[("MOD", l)], w=[("MOD", l)])

    adaln_tiles(0, 96, 6)
    if not (do_gdn and nlayers > 1):
        for l in range(1, nlayers):
            adaln_tiles(l, 96, 6)
    S.barrier()
    for c in range(KC):
        E("act", "mul", X[:, c, :], X[:, c, :], ALPHA, r=[("X", c)], w=[("X", c)])

    def modulate(l, g_sh, g_sc):
        for c in range(KC):
            for s in range(NSEG):
                sl = slice(s * SEGL, (s + 1) * SEGL)
                E("dve", "tensor_scalar", H[:, c, sl], X[:, c, sl], MOD[:, l, g_sc * 16 + c, s:s + 1],
                  MOD[:, l, g_sh * 16 + c, s:s + 1], ALU.mult, ALU.add,
                  r=[("X", c), ("MOD", l)], w=[("H", c)])

    HK = [("H", c) for c in range(KC)]
    bank_sets = ((0, 1, 2), (3, 4, 5))
    bs_i = [0]

    def proj_rows(wt, wk, nk_=KC, rhs=None, rkeys=None, bs=None):
        rhs = H if rhs is None else rhs
        rkeys = HK if rkeys is None else rkeys
        if bs is None:
            bs = bank_sets[bs_i[0] % 2]
            bs_i[0] += 1
        for kc in range(nk_):
            for n, (o, ln) in enumerate(NCH):
                MM(PB[bs[n]][:, 0:ln], wt[:, kc, :], rhs[:, kc, o:o + ln], kc == 0, kc == nk_ - 1,
                   r=[wk] + list(rkeys), w=[("pb", bs[n])])
        return bs

    def colsum_bcast(src_bf, skey, bs=None):
        if bs is None:
            bs = bank_sets[bs_i[0] % 2]
            bs_i[0] += 1
        for n, (o, ln) in enumerate(NCH):
            MM(PB[bs[n]][:, 0:ln], ones_b, src_bf[:, o:o + ln], True, True, r=["CONSTB", skey], w=[("pb", bs[n])])
        return bs

    def out_proj(l, grp):
        for m in range(KC):
            wt, wk = stream_w(w_o[l, grp * 512:(grp + 1) * 512, m * 128:(m + 1) * 128], nk_=4)
            bs = proj_rows(wt, wk, nk_=4, rhs=OTG, rkeys=["OTG"])
            for n, (o, ln) in enumerate(NCH):
                for s in range(o // SEGL, (o + ln) // SEGL):
                    so = s * SEGL - o
                    E("dve", "scalar_tensor_tensor", X[:, m, s * SEGL:(s + 1) * SEGL], PB[bs[n]][:, so:so + SEGL],
                      MOD[:, l, 2 * 16 + m, s:s + 1], X[:, m, s * SEGL:(s + 1) * SEGL], ALU.mult, ALU.add,
                      r=[("pb", bs[n]), ("MOD", l), ("X", m)], w=[("X", m)])

    def layer_norm(l, which, last):
        XB = scr(0, T // 2, BF16)
        XQ = scr(640, T // 2, BF16)
        MEAN = scr(1280, T)
        RSTD = scr(2560, T)
        NMR = scr(3840, T)
        TMP = scr(5120, T)
        sa, sb_ = bank_sets
        for c in range(KC):
            E("act", "copy", XB, X[:, c, :], r=[("X", c)], w=["XB"])
            E("dve", "tensor_tensor", XQ, X[:, c, :], X[:, c, :], ALU.mult, r=[("X", c)], w=["XQ"])
            for n, (o, ln) in enumerate(NCH):
                MM(PB[sa[n]][:, 0:ln], ones_b, XB[:, o:o + ln], c == 0, c == KC - 1, r=["CONSTB", "XB"], w=[("pb", sa[n])])
                MM(PB[sb_[n]][:, 0:ln], ones_b, XQ[:, o:o + ln], c == 0, c == KC - 1, r=["CONSTB", "XQ"], w=[("pb", sb_[n])])
        for n, (o, ln) in enumerate(NCH):
            E("act", "mul", MEAN[:, o:o + ln], PB[sa[n]][:, 0:ln], 1.0 / D, r=[("pb", sa[n])], w=["MEAN"])
            E("dve", "tensor_tensor", TMP[:, o:o + ln], MEAN[:, o:o + ln], MEAN[:, o:o + ln], ALU.mult, r=["MEAN"], w=["TMP"])
            E("dve", "scalar_tensor_tensor", RSTD[:, o:o + ln], PB[sb_[n]][:, 0:ln], 1.0 / D, TMP[:, o:o + ln],
              ALU.mult, ALU.subtract, r=[("pb", sb_[n]), "TMP"], w=["RSTD"])
        rsqrt(RSTD, RSTD, LN_EPS, 1.0, "RSTD", "RSTD")
        E("dve", "scalar_tensor_tensor", NMR, MEAN, -1.0, RSTD, ALU.mult, ALU.mult, r=["MEAN", "RSTD"], w=["NMR"])
        P_ = LNP if last else LNPS
        gi, bi = (0, 1) if which == 1 else (2, 3)
        for c in range(KC):
            E("dve", "tensor_tensor", TMP, X[:, c, :], RSTD, ALU.mult, r=[("X", c), "RSTD"], w=["TMP"])
            E("dve", "tensor_tensor", TMP, TMP, NMR, ALU.add, r=["TMP", "NMR"], w=["TMP"])
            E("act", "activation", X[:, c, :], TMP, AF.Identity, bias=P_[:, l, bi, c:c + 1], scale=P_[:, l, gi, c:c + 1],
              r=["TMP", "LNP", "LNPS"], w=[("X", c)])

    def ffn(l):
        ACTB = scr(0, 11 * T // 2, BF16, shape=(11, T))
        SG = scr(7040, T // 2, BF16)
        for grp in range(4):
            for cc in range(11):
                c = grp * 11 + cc
                wg, wgk = stream_w(w_gu[l, :, c * 128:(c + 1) * 128])
                ba = proj_rows(wg, wgk)
                wu, wuk = stream_w(w_gu[l, :, DFF + c * 128:DFF + (c + 1) * 128])
                bb = proj_rows(wu, wuk)
                for n, (o, ln) in enumerate(NCH):
                    E("act", "activation", SG[:, o:o + ln], PB[ba[n]][:, 0:ln], AF.Silu, r=[("pb", ba[n])], w=["SG"])
                    E("dve", "tensor_tensor", ACTB[:, cc, o:o + ln], SG[:, o:o + ln], PB[bb[n]][:, 0:ln], ALU.mult,
                      r=["SG", ("pb", bb[n])], w=[("ACTB", cc)])
            for m in range(KC):
                wd, wdk = stream_w(w_dn[l, grp * 1408:(grp + 1) * 1408, m * 128:(m + 1) * 128], nk_=11)
                for n, (o, ln) in enumerate(NCH):
                    pb = 6 + (m * 3 + n) % 2
                    for cc in range(11):
                        MM(PB[pb][:, 0:ln], wd[:, cc, :], ACTB[:, cc, o:o + ln], cc == 0, cc == 10,
                           r=[wdk, ("ACTB", cc)], w=[("pb", pb)])
                    for s in range(o // SEGL, (o + ln) // SEGL):
                        so = s * SEGL - o
                        E("dve", "scalar_tensor_tensor", X[:, m, s * SEGL:(s + 1) * SEGL], PB[pb][:, so:so + SEGL],
                          MOD[:, l, 5 * 16 + m, s:s + 1], X[:, m, s * SEGL:(s + 1) * SEGL], ALU.mult, ALU.add,
                          r=[("pb", pb), ("MOD", l), ("X", m)], w=[("X", m)])

    ALG = S.sb("ALG", [128, 2, 16], F32)
    DTB = S.sb("DTB", [128, 2, 16], F32)
    SSQ = S.sb("SSQ", [128, 2, NT], F32)
    S.dma("sp", ALG[:], alog_rep, writes=["ALG"])
    S.dma("sp", DTB[:], dtb_rep, writes=["DTB"])
    E("act", "activation", ALG[:], ALG[:], AF.Exp, r=["ALG"], w=["ALG"])
    E("dve", "tensor_scalar", ALG[:], ALG[:], -1.0, None, ALU.mult, r=["ALG"], w=["ALG"])
    qs_i = [0]

    def qslot():
        b = 3 + qs_i[0] % 4
        qs_i[0] += 1
        return PB[b][:, 0:128], ("pb", b)

    def gdn_layer(l):
        S.barrier()
        GP = 9620
        NBt, GCt, EGt, EDt, EGLA, EGLB = [scr(GP + 160 * i, 160) for i in range(6)]
        GT0, GT1, GT2 = [scr(10580 + 160 * i, 160) for i in range(3)]
        v3 = lambda ap: ap.rearrange("p (a b) -> p a b", a=16)
        wi = wb_i[0] % NWB
        wb_i[0] += 1
        S.dma("pool", WB[wi][:, :, 0:32], w_in[l, :, 4096:4128].rearrange("(k p) j -> p k j", p=128), writes=[("WB", wi)])
        for t in range(NT):
            for kc in range(KC):
                MM(PB[7][:, t * 32:(t + 1) * 32], H[:, kc, t * 128:(t + 1) * 128], WB[wi][:, kc, 0:32], kc == 0, kc == KC - 1,
                   r=[("WB", wi)] + HK, w=[("pb", 7)])
        pv = PB[7][:, 0:320].rearrange("p (t j) -> p t j", t=NT)
        bview = pv[:, :, 0:16].transpose([0, 2, 1])
        aview = pv[:, :, 16:32].transpose([0, 2, 1])
        E("act", "activation", v3(GT0), bview, AF.Sigmoid, r=[("pb", 7)], w=["GT0"])
        E("dve", "tensor_scalar", NBt, GT0, -1.0, None, ALU.mult, r=["GT0"], w=["NB"])
        E("dve", "tensor_tensor", v3(GT1), aview, DTB[:, l, :].unsqueeze(2).to_broadcast([128, 16, NT]), ALU.add,
          r=[("pb", 7), "DTB"], w=["GT1"])
        E("act", "activation", GT1, GT1, AF.Exp, r=["GT1"], w=["GT1"])
        E("act", "activation", GT1, GT1, AF.Ln, bias=EPS[:, 3:4], r=["GT1", "EPS"], w=["GT1"])
        E("dve", "tensor_tensor", v3(GT2), v3(GT1), ALG[:, l, :].unsqueeze(2).to_broadcast([128, 16, NT]), ALU.mult,
          r=["GT1", "ALG"], w=["GT2"])
        MM(PB[6][:, 0:80], CONST[:, C_MIF, :], GT2[:, 0:80], True, True, r=["CONST", "GT2"], w=[("pb", 6)])
        MM(PB[6][:, 80:160], CONST[:, C_MIB, :], GT2[:, 80:160], True, True, r=["CONST", "GT2"], w=[("pb", 6)])
        MM(PB[6][:, 160:320], CONST[:, C_BLK, :], GT2, True, True, r=["CONST", "GT2"], w=[("pb", 6)])
        MM(PB[6][:, 320:480], CONST[:, C_H0, :], GT2, True, True, r=["CONST", "GT2"], w=[("pb", 6)])
        MM(PB[5][:, 0:160], CONST[:, C_H1, :], GT2, True, True, r=["CONST", "GT2"], w=[("pb", 5)])
        E("dve", "tensor_copy", GCt, PB[6][:, 0:160], r=[("pb", 6)], w=["GC"])
        E("act", "activation", EGt, PB[6][:, 0:160], AF.Exp, r=[("pb", 6)], w=["EG"])
        E("dve", "tensor_tensor", GT0, PB[6][:, 160:320], GCt, ALU.subtract, r=[("pb", 6), "GC", "GT0"], w=["GT0"])
        E("act", "activation", EDt, GT0, AF.Exp, r=["GT0"], w=["ED"])
        E("act", "activation", EGLA, PB[6][:, 320:480], AF.Exp, r=[("pb", 6)], w=["EGL"])
        E("act", "activation", EGLB, PB[5][:, 0:160], AF.Exp, r=[("pb", 5)], w=["EGL"])
        S.barrier()
        for h in range(8):
            gdn_head(l, h, NBt, GCt, EGt, EDt, EGLA, EGLB)
            if h % 4 == 3:
                out_proj(l, h // 4)
        S.barrier()

    def gdn_head(l, h, NBt, GCt, EGt, EDt, EGLA, EGLB):
        CP = scr(0, 1300)
        CP3 = CP.rearrange("p (s w) -> p s w", s=NSEG)
        ACC = scr(1300, T)
        ACC3 = ACC.rearrange("p (s w) -> p s w", s=NSEG)
        SIL = scr(2580, T)
        SQB = scr(3860, T // 2, BF16)
        RN = scr(4500, T)
        QN = scr(5780, T // 2, BF16)
        KN = scr(6420, T // 2, BF16)
        VTM = scr(7060, T, shape=(NT, 128))
        KDF = scr(8340, T // 2, BF16, shape=(NT, 128))
        KDB = scr(8980, T // 2, BF16, shape=(NT, 128))
        R5 = scr(0, T, BF16, shape=(20, 128))
        ATT = scr(1280, T, BF16, shape=(20, 128))
        OTM = scr(2560, T, shape=(NT, 128))
        ZS = scr(3840, T, shape=(NT, 128))
        DIAG, EDM, TMPM, P0, P0T, P1, P1T, RR = [scr(11264 + 128 * i, 128) for i in range(8)]
        NR = scr(5120, 64, BF16)
        VN = scr(5184, 64, BF16)
        TMPS = scr(5248, 128)
        SF = [scr(5376, 128), scr(5504, 128)]
        SBF = [scr(5632, 64, BF16), scr(5696, 64, BF16)]
        S.barrier()
        streams = ((h * 128, h), (1024 + h * 128, 8 + h), (2048 + h * 128, 16 + h))

        def s1_proj(si, bs):
            col, ci = streams[si]
            wt, wk = stream_w(w_in[l, :, col:col + 128])
            return proj_rows(wt, wk, bs=bs)

        def s1_elem(si, bs, cs):
            col, ci = streams[si]
            E("dve", "memset", CP, 0.0, w=["CP"])
            for n, (o, ln) in enumerate(NCH):
                for s in range(o // SEGL, (o + ln) // SEGL):
                    so = s * SEGL - o
                    E("act", "copy", CP3[:, s, 2:258], PB[bs[n]][:, so:so + SEGL], r=[("pb", bs[n])], w=["CP"])
            E("dve", "tensor_scalar", CP3[:, 2:5, 0:2], CP3[:, 1:4, 256:258], FLG[:, 0:1], None, ALU.mult, r=["CP", "FLG"], w=["CP"])
            E("dve", "tensor_scalar", CP3[:, 1:4, 258:260], CP3[:, 2:5, 2:4], FLG[:, 0:1], None, ALU.mult, r=["CP", "FLG"], w=["CP"])
            E("dve", "tensor_scalar", ACC3, CP3[:, :, 0:256], CNV[:, l, ci, 0:1], None, ALU.mult, r=["CP", "CNV"], w=["ACC"])
            for j in range(1, 5):
                E("dve", "scalar_tensor_tensor", ACC3, CP3[:, :, j:j + 256], CNV[:, l, ci, j:j + 1], ACC3, ALU.mult, ALU.add,
                  r=["CP", "CNV", "ACC"], w=["ACC"])
            E("act", "activation", SIL, ACC, AF.Silu, r=["ACC"], w=["SIL"])
            if si < 2:
                E("dve", "tensor_tensor", SQB, SIL, SIL, ALU.mult, r=["SIL"], w=["SQB"])
                b2 = colsum_bcast(SQB, "SQB", bs=cs)
                for n, (o, ln) in enumerate(NCH):
                    rsqrt(RN[:, o:o + ln], PB[b2[n]][:, 0:ln], RMS_EPS, 1.0, ("pb", b2[n]), "RN")
            if si == 0:
                E("dve", "scalar_tensor_tensor", QN, SIL, 128.0 ** -0.5, RN, ALU.mult, ALU.mult, r=["SIL", "RN"], w=["QN"])
            elif si == 1:
                E("dve", "tensor_tensor", SIL, SIL, RN, ALU.mult, r=["SIL", "RN"], w=["SIL"])
                E("act", "copy", KN, SIL, r=["SIL"], w=["KN"])
                for t in range(NT):
                    q, qk = qslot()
                    TR(q, SIL[:, t * 128:(t + 1) * 128], ident, r=["SIL", "CONST"], w=[qk])
                    cf, cb = h * NT + t, (8 + h) * NT + t
                    E("dve", "tensor_scalar", KDF[:, t, :], q, EDt[:, cf:cf + 1], None, ALU.mult, r=[qk, "ED"], w=["KDF"])
                    E("dve", "tensor_scalar", KDB[:, t, :], q, EDt[:, cb:cb + 1], None, ALU.mult, r=[qk, "ED"], w=["KDB"])
            else:
                for t in range(NT):
                    q, qk = qslot()
                    TR(q, SIL[:, t * 128:(t + 1) * 128], ident, r=["SIL", "CONST"], w=[qk])
                    E("act", "copy", VTM[:, t, :], q, r=[qk], w=["VTM"])

        sA, sB = bank_sets
        s1_proj(0, sA)
        s1_proj(1, sB)
        s1_elem(0, sA, sA)
        s1_proj(2, sA)
        s1_elem(1, sB, sB)
        s1_elem(2, sA, None)
        S.barrier()
        rot = [0]

        def rslot():
            bnk = 2 + rot[0] % 5
            rot[0] += 1
            return PB[bnk][:, 0:128], ("pb", bnk)

        ibase = [2560, 3200, 3840, 4480, 5120, 11264]

        def inst_gen(t, d, ib, gq, kq, gk):
            Pa, Pb_, Pc, Pd, RR = [scr(ibase[ib] + 128 * i, 128) for i in range(5)]
            ka, kb_, kc_, kd, kr = [("I", ib, i) for i in range(5)]
            col = (d * 8 + h) * NT + t
            mS = CONST[:, C_MSF if d == 0 else C_MSB, :]
            mI = CONST[:, C_MIF if d == 0 else C_MIB, :]
            E("dve", "tensor_scalar", Pc, ident, GCt[:, col:col + 1], None, ALU.mult, r=["CONST", "GC"], w=[kc_])
            rq, rk = rslot()
            MM(rq, ones_f, Pc, True, True, r=["CONST", kc_], w=[rk])
            E("dve", "tensor_scalar", Pd, rq, GCt[:, col:col + 1], 0.0, ALU.subtract, ALU.min, r=[rk, "GC"], w=[kd])
            yield
            E("act", "activation", Pd, Pd, AF.Exp, r=[kd], w=[kd])
            E("dve", "tensor_tensor", Pd, Pd, mI, ALU.mult, r=[kd, "CONST"], w=[kd])
            yield
            E("dve", "scalar_tensor_tensor", Pa, gq, NBt[:, col:col + 1], Pd, ALU.mult, ALU.mult, r=[gk, "NB", kd], w=[ka])
            E("dve", "tensor_tensor", ATT[:, d * NT + t, :], kq, Pd, ALU.mult, r=[gk, kd], w=[("ATT", d, t)])
            E("dve", "tensor_tensor", Pa, Pa, mS, ALU.mult, r=[ka, "CONST"], w=[ka])
            yield
            tq, tk = rslot()
            TR(tq, Pa, ident, r=[ka, "CONST"], w=[tk])
            E("act", "copy", Pb_, tq, r=[tk], w=[kb_])
            E("dve", "tensor_tensor", RR, Pa, ident, ALU.add, r=[ka, "CONST"], w=[kr])
            yield
            st = {"cur": (Pa, ka), "curT": (Pb_, kb_), "nxt": (Pc, kc_), "nxtT": (Pd, kd)}

            def square(n):
                cur, curT, nxt, nxtT = st["cur"], st["curT"], st["nxt"], st["nxtT"]
                if n < 5:
                    aq, ak = rslot()
                    MM(aq, curT[0], cur[0], True, True, r=[curT[1], cur[1]], w=[ak])
                    E("act", "copy", nxt[0], aq, r=[ak], w=[nxt[1]])
                bq, bk = rslot()
                MM(bq, cur[0], curT[0], True, True, r=[curT[1], cur[1]], w=[bk])
                E("dve", "tensor_copy", nxtT[0], bq, r=[bk], w=[nxtT[1]])
                st["cur"], st["curT"], st["nxt"], st["nxtT"] = nxt, nxtT, cur, curT

            def update(n):
                PT = st["curT"]
                cq, ck = rslot()
                MM(cq, PT[0], RR, True, True, r=[PT[1], kr], w=[ck])
                if n < 5:
                    E("dve", "tensor_tensor", RR, RR, cq, ALU.add, r=[kr, ck], w=[kr])
                else:
                    E("dve", "tensor_tensor", R5[:, d * NT + t, :], RR, cq, ALU.add, r=[kr, ck], w=[("R5", d, t)])

            square(1)
            yield
            for n in range(1, 6):
                update(n)
                if n < 5:
                    square(n + 1)
                yield

        for tiles in ((0, 1, 2), (3, 4, 5), (6, 7), (8, 9)):
            gens = []
            for j, t in enumerate(tiles):
                tl = slice(t * 128, (t + 1) * 128)
                gb = 0 if j < 2 else 1
                gq = PB[gb][:, (2 * (j % 2)) * 128:(2 * (j % 2) + 1) * 128]
                kq = PB[gb][:, (2 * (j % 2) + 1) * 128:(2 * (j % 2) + 2) * 128]
                MM(gq, KN[:, tl], KN[:, tl], True, True, r=["KN"], w=[("pb", gb)])
                MM(kq, KN[:, tl], QN[:, tl], True, True, r=["KN", "QN"], w=[("pb", gb)])
                for d in range(2):
                    gens.append(inst_gen(t, d, 2 * j + d, gq, kq, ("pb", gb)))
            while gens:
                for g_ in list(gens):
                    try:
                        next(g_)
                    except StopIteration:
                        gens.remove(g_)
        S.barrier()
        NRs = [NR, scr(11264, 64, BF16)]
        VNs = [VN, scr(11328, 64, BF16)]
        TMPs = [TMPS, scr(11392, 128)]
        for d in range(2):
            E("dve", "memset", NRs[d], 0.0, w=[("NR", d)])
            E("dve", "memset", VNs[d], 0.0, w=[("VN", d)])
        E("dve", "memset", OTM.rearrange("p a b -> p (a b)"), 0.0, w=[("OTM", t) for t in range(NT)])
        crot = [0]

        def cslot():
            bnk = crot[0] % 7
            crot[0] += 1
            return PB[bnk][:, 0:128], ("pb", bnk)

        order = [[(t, e) for t in range(NT) for e in (0, 1)], [(t, e) for t in range(NT - 1, -1, -1) for e in (1, 0)]]
        EGLx = (EGLA, EGLB)
        def chain_step(k, d):
            t, e = order[d][k]
            s = t // 2
            tl = slice(t * 128, (t + 1) * 128)
            rows = slice(e * 64, (e + 1) * 64)
            col = (d * 8 + h) * NT + t
            SK, SBK, NK, VK, TK = ("SF", d), ("SBF", d), ("NR", d), ("VN", d), ("TMPS", d)
            NR_, VN_, TM_ = NRs[d], VNs[d], TMPs[d]
            seg_start = (t % 2 == 0 and e == 0) if d == 0 else (t % 2 == 1 and e == 1)
            seg_end = (t % 2 == 1 and e == 1) if d == 0 else (t % 2 == 0 and e == 0)
            if seg_start:
                if s == 0:
                    E("dve", "memset", SF[d], 0.0, w=[SK])
                elif (d == 0 and s == 1) or (d == 1 and s == 4):
                    S.dma("sp", SF[d], sinit[l, d, h], writes=[SK])
                else:
                    E("dve", "tensor_scalar", SF[d], SF[d], FLG[:, 0:1], None, ALU.mult, r=[SK, "FLG"], w=[SK])
                E("act", "copy", SBF[d], SF[d], r=[SK], w=[SBK])
            q1, k1 = cslot()
            MM(q1, KN[:, tl], SBF[d], True, True, r=["KN", SBK], w=[k1])
            E("dve", "scalar_tensor_tensor", NR_[rows, :], q1[rows, :], EGt[rows, col:col + 1], VTM[rows, t, :], ALU.mult, ALU.subtract,
              r=[k1, "EG", "VTM"], w=[NK])
            q3, k3 = cslot()
            MM(q3, QN[:, tl], SBF[d], True, True, r=["QN", SBK], w=[k3])
            E("act", "activation", TM_[rows, :], q3[rows, :], AF.Identity, scale=EGt[rows, col:col + 1], r=[k3, "EG"], w=[TK])
            yield
            q2, k2 = cslot()
            MM(q2, R5[:, d * NT + t, :], NR_, True, True, r=[("R5", d, t), NK], w=[k2])
            E("dve", "tensor_scalar", VN_[rows, :], q2[rows, :], NBt[rows, col:col + 1], None, ALU.mult, r=[k2, "NB"], w=[VK])
            yield
            q5, k5 = cslot()
            KD = KDF if d == 0 else KDB
            MM(q5, KD[rows, t, :], VN_[rows, :], True, True, r=["KDF", "KDB", VK], w=[k5])
            E("dve", "scalar_tensor_tensor", SBF[d], SF[d], EGLx[e][:, col:col + 1], q5, ALU.mult, ALU.add, r=[SK, "EGL", k5], w=[SBK])
            E("dve", "scalar_tensor_tensor", SF[d], SF[d], EGLx[e][:, col:col + 1], q5, ALU.mult, ALU.add, r=[SK, "EGL", k5], w=[SK])
            yield
            q4, k4 = cslot()
            MM(q4, ATT[:, d * NT + t, :], VN_, True, True, r=[("ATT", d, t), VK], w=[k4])
            E("dve", "tensor_tensor", TM_[rows, :], TM_[rows, :], q4[rows, :], ALU.add, r=[TK, k4], w=[TK])
            E("dve", "tensor_tensor", OTM[rows, t, :], OTM[rows, t, :], TM_[rows, :], ALU.add, r=[TK, ("OTM", t)], w=[("OTM", t)])
            if seg_end:
                stg = scr(11520 + 128 * (2 * d + s % 2), 128)
                E("act", "copy", stg, SF[d], r=[SK], w=[("STG", d, s % 2)])
                S.dma("sp", nst[l, s, d, h], stg, reads=[("STG", d, s % 2)], writes=[("nst", l, s, d, h)])
                outkeys.append(("nst", l, s, d, h))

        wz, wzk = stream_w(w_in[l, :, 3072 + h * 128:3072 + (h + 1) * 128])
        wb_pinned.add(wzk[1])
        for k in range(2 * NT):
            if l + 1 < nlayers:
                adaln_tiles(l + 1, 1, 7)
            if k % 2 == 0:
                tz = k // 2
                zq, zk = cslot()
                for kc in range(KC):
                    MM(zq, H[:, kc, tz * 128:(tz + 1) * 128], wz[:, kc, :], kc == 0, kc == KC - 1, r=[wzk] + HK, w=[zk])
                E("act", "activation", ZS[:, tz, :], zq, AF.Silu, r=[zk], w=["ZS"])
            gens = [chain_step(k, 0), chain_step(k, 1)]
            while gens:
                for g_ in list(gens):
                    try:
                        next(g_)
                    except StopIteration:
                        gens.remove(g_)
        wb_pinned.discard(wzk[1])
        S.barrier()
        for t in range(NT):
            E("act", "activation", TMPS, OTM[:, t, :], AF.Square, accum_out=SSQ[:, 0, t:t + 1], r=[("OTM", t)], w=["TMPS", "SSQ"])
        rsqrt(SSQ[:, 1, :], SSQ[:, 0, :], RMS_EPS, 1.0 / 128, "SSQ", "SSQ")
        OK_ = [("OTM", t) for t in range(NT)]
        E("dve", "tensor_tensor", OTM, OTM, SSQ[:, 1, :].unsqueeze(2).to_broadcast([128, NT, 128]), ALU.mult, r=OK_ + ["SSQ"], w=OK_)
        E("dve", "tensor_tensor", OTM, OTM, GNW[:, l, :].unsqueeze(1).to_broadcast([128, NT, 128]), ALU.mult, r=OK_ + ["GNW"], w=OK_)
        E("dve", "tensor_tensor", OTM, OTM, ZS, ALU.mult, r=OK_ + ["ZS"], w=OK_)
        for t in range(NT):
            oq, ok = qslot()
            TR(oq, OTM[:, t, :], ident, r=[("OTM", t), "CONST"], w=[ok])
            E("act", "copy", OTG[:, h % 4, t * 128:(t + 1) * 128], oq, r=[ok], w=["OTG"])

    def qk_prep(l, col, which, raw, sqb, rn, nrm, outb, ropec, ropes):
        wt, wk = stream_w(w_in[l, :, col:col + 128])
        bs = proj_rows(wt, wk)
        b2 = bank_sets[bs_i[0] % 2]
        bs_i[0] += 1
        ok_ = ("A_out", id(outb))

        def chunk(n, o, ln):
            c = slice(o, o + ln)
            kraw, ksq, krn, knrm = ("A_raw", n), ("A_sq", n), ("A_rn", n), ("A_nrm", n)
            E("act", "copy", raw[:, c], PB[bs[n]][:, 0:ln], r=[("pb", bs[n])], w=[kraw])
            yield
            E("dve", "tensor_tensor", sqb[:, c], raw[:, c], raw[:, c], ALU.mult, r=[kraw], w=[ksq])
            yield
            MM(PB[b2[n]][:, 0:ln], ones_b, sqb[:, c], True, True, r=["CONSTB", ksq], w=[("pb", b2[n])])
            rsqrt(rn[:, c], PB[b2[n]][:, 0:ln], RMS_EPS, 1.0 / 128, ("pb", b2[n]), krn)
            yield
            E("dve", "scalar_tensor_tensor", nrm[:, c], raw[:, c], QKW[:, l, which:which + 1], rn[:, c], ALU.mult, ALU.mult,
              r=[kraw, "QKW", krn], w=[knrm, "A_nrm"])
            yield
            MM(PB[bs[n]][:, 0:ln], CONST[:, C_PERM, :], nrm[:, c], True, True, r=["CONST", knrm], w=[("pb", bs[n])])
            E("dve", "tensor_tensor", rn[:, c], PB[bs[n]][:, 0:ln], ropes[:, c], ALU.mult, r=[("pb", bs[n]), "ROPE"], w=[krn])
            E("dve", "tensor_tensor", raw[:, c], nrm[:, c], ropec[:, c], ALU.mult, r=[knrm, "ROPE"], w=[kraw])
            yield
            E("dve", "tensor_tensor", outb[:, c], raw[:, c], rn[:, c], ALU.add, r=[kraw, krn], w=[ok_, ("A_outc", id(outb), n)])

        gens = [chunk(n, o, ln) for n, (o, ln) in enumerate(NCH)]
        while gens:
            for g_ in list(gens):
                try:
                    next(g_)
                except StopIteration:
                    gens.remove(g_)

    def attn_layer(l):
        S.barrier()
        ROPEC = scr(0, T)
        ROPES = scr(1280, T)
        RAW = scr(2560, T)
        SQB = scr(3840, T // 2, BF16)
        RN = scr(4480, T)
        NRM = scr(5760, T)
        KRB = scr(7040, T // 2, BF16)
        VTM = scr(7680, T // 2, BF16, shape=(NT, 128))
        CKB = scr(8320, 256, BF16)
        CVB = scr(8576, 256, BF16, shape=(4, 128))
        QRB = scr(8832, T // 2, BF16)
        VA = scr(9472, T)
        PT = [scr(10752 + 256 * i, 256, BF16) for i in range(4)]
        S.dma("sp", ROPEC, ropeC_in, writes=["ROPE"])
        S.dma("sp", ROPES, ropeS_in, writes=["ROPE"])
        sc = 128.0 ** -0.5
        kq, oq = id(KRB), id(QRB)
        for g in range(2):
            qk_prep(l, 5152 + g * 128, 1, RAW, SQB, RN, NRM, KRB, ROPEC, ROPES)
            S.dma("sp", nk[l, g], NRM, reads=["A_nrm"], writes=[("nk", l, g)])
            outkeys.append(("nk", l, g))
            wt, wk = stream_w(w_in[l, :, 5408 + g * 128:5408 + (g + 1) * 128])
            bs = proj_rows(wt, wk)
            for n, (o, ln) in enumerate(NCH):
                E("act", "copy", VA[:, o:o + ln], PB[bs[n]][:, 0:ln], r=[("pb", bs[n])], w=["A_va"])
            S.dma("sp", nv[l, g], VA, reads=["A_va"], writes=[("nv", l, g)])
            outkeys.append(("nv", l, g))
            for t in range(NT):
                pb, qd = 6 + (t // 4) % 2, t % 4
                TR(PB[pb][:, qd * 128:(qd + 1) * 128], VA[:, t * 128:(t + 1) * 128], ident, r=["A_va", "CONST"], w=[("pb", pb)])
                E("act", "copy", VTM[:, t, :], PB[pb][:, qd * 128:(qd + 1) * 128], r=[("pb", pb)], w=["A_vtm"])
            S.dma("pool", CKB, ckT[l, g], writes=["A_ck"])
            S.dma("pool", CVB, cvv[l, g].rearrange("(b p) d -> p b d", p=128), writes=["A_cv"])
            for hl in range(4):
                hq = 4 * g + hl
                qk_prep(l, 4128 + hq * 128, 0, RAW, SQB, RN, NRM, QRB, ROPEC, ROPES)
                jobs = [(0, 256, [("l", kb) for kb in (0, 1)])]
                for qc in range(2):
                    jobs.append((256 + qc * 512, 512, [("l", kb) for kb in range(2, 10)] + [("c", cb) for cb in range(4)]))
                for (q0, qn, blocks) in jobs:
                    nb = len(blocks)
                    STB = (0, 1, 4, 5)

                    def emit_st(bi):
                        kind, kb = blocks[bi]
                        sl_ = bi % 4
                        pb = STB[sl_]
                        if kind == "l":
                            klhs, kr = KRB[:, kb * 128:(kb + 1) * 128], ("A_out", kq)
                        else:
                            klhs, kr = CKB[:, kb * 128:(kb + 1) * 128], "A_ck"
                        MM(PB[pb][:, 0:qn], klhs, QRB[:, q0:q0 + qn], True, True, r=[kr, ("A_out", oq)], w=[("pb", pb)])
                        for hh in range(qn // 256):
                            qs = (q0 + hh * 256) // 256
                            same = (kind == "l") and (kb // 2 == qs)
                            E("act", "activation", PT[sl_][:, hh * 256:(hh + 1) * 256], PB[pb][:, hh * 256:(hh + 1) * 256], AF.Exp,
                              bias=(EPS[:, 2:3] if same else FLG[:, 1:2]), scale=sc, r=[("pb", pb), "FLG", "EPS"], w=[("PT", sl_)])

                    def emit_pv(bi):
                        kind, kb = blocks[bi]
                        sl_ = bi % 4
                        if kind == "l":
                            vlhs, vr = VTM[:, kb, :], "A_vtm"
                        else:
                            vlhs, vr = CVB[:, kb, :], "A_cv"
                        MM(PB[2][:, 0:qn], vlhs, PT[sl_][:, 0:qn], bi == 0, bi == nb - 1, r=[vr, ("PT", sl_)], w=[("pb", 2)])
                        MM(PB[3][:, 0:qn], ones_b, PT[sl_][:, 0:qn], bi == 0, bi == nb - 1, r=["CONSTB", ("PT", sl_)], w=[("pb", 3)])

                    LA = 2
                    for bi in range(min(LA, nb)):
                        emit_st(bi)
                    for bi in range(nb):
                        if bi + LA < nb:
                            emit_st(bi + LA)
                        emit_pv(bi)
                    E("dve", "reciprocal", RAW[:, 0:qn], PB[3][:, 0:qn], r=[("pb", 3)], w=[("A_raw", 0)])
                    E("dve", "tensor_tensor", OTG[:, hl, q0:q0 + qn], PB[2][:, 0:qn], RAW[:, 0:qn], ALU.mult,
                      r=[("pb", 2), ("A_raw", 0)], w=["OTG"])
            out_proj(l, 2 + g)
        S.barrier()

    for l in range(nlayers):
        modulate(l, 0, 1)
        if do_gdn:
            gdn_layer(l)
        if do_attn:
            attn_layer(l)
        if not (do_gdn or do_attn):
            pass
        S.barrier()
        layer_norm(l, 1, False)
        modulate(l, 3, 4)
        S.barrier()
        ffn(l)
        S.barrier()
        layer_norm(l, 2, l == nlayers - 1)
        S.barrier()

    for c in range(KC):
        S.dma("sp", yT[c * 128:(c + 1) * 128, :], X[:, c, :], reads=[("X", c)], writes=[("yT", c)])
        outkeys.append(("yT", c))
    S.finish(final_keys=outkeys)
    return nc


def _core_segments(core):
    if core < 6:
        return [("p", 5 * core + s, 0) for s in range(5)]
    b = core - 6
    return [("p", 30 + b, 0)] + [("s", b, q) for q in range(4)]


def _rope_tables(core):
    C = np.ones((128, T), np.float32)
    Sg = np.zeros((128, T), np.float32)
    if core >= 6:
        pos = np.arange(1024)
        row = (pos // 64).astype(np.float32)
        col = (pos % 64).astype(np.float32)
        inv = (np.float32(10000.0) ** (-np.arange(0, 64, 2, dtype=np.float32) / np.float32(64))).astype(np.float32)
        d = np.arange(128)
        ang = np.where((d < 64)[:, None], row[None, :] * inv[d % 32][:, None], col[None, :] * inv[d % 32][:, None])
        ang = ang.astype(np.float32)
        sign = np.where((d % 64) < 32, -1.0, 1.0).astype(np.float32)[:, None]
        C[:, 256:] = np.cos(ang)
        Sg[:, 256:] = np.sin(ang) * sign
    return C, Sg


def make_in_maps(inp):
    f = np.float32
    g = lambda k: np.asarray(inp[k], dtype=f)
    x_prompt, x_sample = g("x_prompt"), g("x_sample")
    cache_k, cache_v, state_gdn = g("cache_k"), g("cache_v"), g("state_gdn")
    c, c_ctx = g("c"), g("c_ctx")
    shared = {
        "w_ada": g("w_ada"), "w_in": g("w_in"), "w_o": g("w_o"), "w_gu": g("w_gate_up"), "w_dn": g("w_down"),
        "b_adaT": np.ascontiguousarray(g("b_ada").reshape(2, 96, 128).transpose(2, 0, 1)),
        "convT": np.ascontiguousarray(g("conv_w").reshape(2, 5, 24, 128).transpose(3, 0, 2, 1)),
        "alog_rep": np.ascontiguousarray(np.broadcast_to(g("a_log").reshape(1, 2, 16), (128, 2, 16))),
        "dtb_rep": np.ascontiguousarray(np.broadcast_to(g("dt_bias").reshape(1, 2, 16), (128, 2, 16))),
        "gnw_rep": np.ascontiguousarray(np.broadcast_to(g("gdn_norm_w").reshape(1, 2, 128), (128, 2, 128))),
        "qknw": np.ascontiguousarray(np.stack([g("q_norm_w"), g("k_norm_w")], 0).transpose(2, 1, 0)),
        "lnp": np.ascontiguousarray(np.stack([g("ln1_g"), g("ln1_b"), g("ln2_g"), g("ln2_b")], 0)
                                    .reshape(4, 2, 16, 128).transpose(3, 1, 0, 2)),
        "consts": make_consts(),
    }
    maps = []
    for core in range(8):
        segs = _core_segments(core)
        xs, cs = [], []
        for kind, i, q in segs:
            if kind == "p":
                xs.append(x_prompt[i])
                cs.append(c_ctx)
            else:
                xs.append(x_sample[i, q * 256:(q + 1) * 256])
                cs.append(c[i])
        m = dict(shared)
        m["xT"] = np.ascontiguousarray(np.concatenate(xs, 0).T)
        m["condT"] = np.ascontiguousarray(np.stack(cs, 0).reshape(5, 16, 128).transpose(2, 1, 0))
        if core >= 6:
            b = core - 6
            m["ckT"] = np.ascontiguousarray(cache_k[b].transpose(0, 2, 3, 1))
            m["cvv"] = np.ascontiguousarray(cache_v[b].transpose(0, 2, 1, 3))
            m["sinit"] = np.ascontiguousarray(state_gdn[b])
            m["flags"] = np.ascontiguousarray(np.broadcast_to(np.array([1.0, 0.0], f), (128, 2)))
        else:
            m["ckT"] = np.zeros((2, 2, 128, 512), f)
            m["cvv"] = np.zeros((2, 2, 512, 128), f)
            m["sinit"] = np.zeros((2, 2, 8, 128, 128), f)
            m["flags"] = np.ascontiguousarray(np.broadcast_to(np.array([0.0, NEG], f), (128, 2)))
        m["ropeC"], m["ropeS"] = _rope_tables(core)
        maps.append(m)
    return maps


def assemble(results):
    f = np.float32
    y_p = np.zeros((32, 256, D), f)
    y_s = np.zeros((2, 1024, D), f)
    ck = np.zeros((32, 2, 256, 2, 128), f)
    cv = np.zeros((32, 2, 256, 2, 128), f)
    st = np.zeros((32, 2, 2, 8, 128, 128), f)
    for core in range(8):
        r = results[core]
        y = np.asarray(r["yT"]).T
        nk_ = np.asarray(r["nk"])
        nv_ = np.asarray(r["nv"])
        ns_ = np.asarray(r["nst"])
        for s, (kind, i, q) in enumerate(_core_segments(core)):
            sl = slice(s * 256, (s + 1) * 256)
            if kind == "p":
                y_p[i] = y[sl]
                ck[i] = nk_[:, :, :, sl].transpose(0, 3, 1, 2)
                cv[i] = nv_[:, :, :, sl].transpose(0, 3, 1, 2)
                st[i] = ns_[:, s]
            else:
                y_s[i, q * 256:(q + 1) * 256] = y[sl]
    return y_p, y_s, ck, cv, st


_NC_CACHE = {}


def kernel(**inputs):
    maps = make_in_maps(inputs)
    if "nc" not in _NC_CACHE:
        _NC_CACHE["nc"] = build_program(json.loads(os.environ.get("KCFG", "{}")))
    res = run_bass_kernel_spmd(_NC_CACHE["nc"], maps, core_ids=list(range(8)))
    return assemble(res.results)
```
